# Optimizing a Trainium2 kernel written in Bass

```python
import jax, jax.numpy as jnp
from jax import lax
import numpy as np

D_MODEL = 1024
BATCH = 8
SEQ = 8192
DEPTH = 2
DEC_BATCH = 4
DEC_SEQ = 4096
PAST_LEN = 128

HEAD_DIM = 64
A_Q_HEADS = 8
A_KV_HEADS = 2
WINDOW = 128
A_BLOCK = 128
B_HEADS = 4
GRID_W = 64
NB_ROWS_MAX = 8
NB_COLS = 16
M_HEADS = 4
N_MEM = 256
D_FF = 4 * D_MODEL
EPS = 1e-6

A_Q = A_Q_HEADS * HEAD_DIM
A_KV = A_KV_HEADS * HEAD_DIM
B_W = B_HEADS * HEAD_DIM
M_W = M_HEADS * HEAD_DIM
MIX_W = A_Q + B_W + M_W
IN_W = A_Q + 2 * A_KV + 3 * B_W + M_W
SPLITS = [A_Q, A_Q + A_KV, A_Q + 2 * A_KV, A_Q + 2 * A_KV + B_W,
          A_Q + 2 * A_KV + 2 * B_W, A_Q + 2 * A_KV + 3 * B_W]

kernel_name = "hybrid_window_natten_memory_encoder"


def _rmsnorm(x, g):
    xf = x.astype(jnp.float32)
    y = xf * lax.rsqrt(jnp.mean(jnp.square(xf), axis=-1, keepdims=True) + EPS)
    return (y * g.astype(jnp.float32)).astype(x.dtype)


def _alibi_slopes(n):
    return jnp.exp2(-8.0 * jnp.arange(1, n + 1, dtype=jnp.float32) / n)


def _window_gqa(q, k, v, sink):
    B, S = q.shape[0], q.shape[1]
    nb = S // A_BLOCK
    G = A_Q_HEADS // A_KV_HEADS
    qb = q.reshape(B, nb, A_BLOCK, A_KV_HEADS, G, HEAD_DIM)
    pad = ((0, 0), (A_BLOCK, A_BLOCK), (0, 0), (0, 0))

    def band(t):
        tb = jnp.pad(t, pad).reshape(B, nb + 2, A_BLOCK, A_KV_HEADS, HEAD_DIM)
        return jnp.concatenate([tb[:, :-2], tb[:, 1:-1], tb[:, 2:]], axis=2)

    kb, vb = band(k), band(v)
    s = jnp.einsum('bnqhgd,bnshd->bnhgqs', qb, kb,
                   preferred_element_type=jnp.float32) * (HEAD_DIM ** -0.5)
    qi = jnp.arange(A_BLOCK)
    si = jnp.arange(3 * A_BLOCK)
    dist = jnp.abs(si[None, :] - A_BLOCK - qi[:, None]).astype(jnp.float32)
    kpos = jnp.arange(nb)[:, None] * A_BLOCK - A_BLOCK + si[None, :]
    valid = (dist <= WINDOW)[None] & ((kpos >= 0) & (kpos < S))[:, None, :]
    slopes = _alibi_slopes(A_Q_HEADS).reshape(A_KV_HEADS, G)
    s = s - slopes[:, :, None, None] * dist
    s = jnp.where(valid[None, :, None, None], s, -jnp.inf)
    sk = sink.astype(jnp.float32).reshape(A_KV_HEADS, G)[None, None, :, :, None, None]
    m = jnp.maximum(jnp.max(s, axis=-1, keepdims=True), sk)
    p = jnp.exp(s - m)
    p = p / (jnp.sum(p, axis=-1, keepdims=True) + jnp.exp(sk - m))
    o = jnp.einsum('bnhgqs,bnshd->bnqhgd', p.astype(v.dtype), vb)
    return o.reshape(B, S, A_Q)


def _neighborhood_attn(q, k, v, rpb):
    B, S = q.shape[0], q.shape[1]
    rows = S // GRID_W
    kh = min(NB_ROWS_MAX, rows)
    r = jnp.arange(rows)
    ridx = jnp.clip(r - kh // 2, 0, rows - kh)[:, None] + jnp.arange(kh)[None, :]
    c = jnp.arange(GRID_W)
    cstart = jnp.clip(c - NB_COLS // 2, 0, GRID_W - NB_COLS)
    cvalid = (c[None, :] >= cstart[:, None]) & (c[None, :] < cstart[:, None] + NB_COLS)
    qg = q.reshape(B, rows, GRID_W, B_HEADS, HEAD_DIM)
    kg = k.reshape(B, rows, GRID_W, B_HEADS, HEAD_DIM)[:, ridx]
    vg = v.reshape(B, rows, GRID_W, B_HEADS, HEAD_DIM)[:, ridx]
    s = jnp.einsum('brqhd,brkwhd->brhqkw', qg, kg,
                   preferred_element_type=jnp.float32) * (HEAD_DIM ** -0.5)
    dr = ridx - r[:, None] + (NB_ROWS_MAX - 1)
    dc = jnp.clip(c[None, :] - c[:, None] + (NB_COLS - 1), 0, 2 * NB_COLS - 2)
    bias = rpb.astype(jnp.float32)[:, dr[:, None, :, None], dc[None, :, None, :]]
    s = s + jnp.transpose(bias, (1, 0, 2, 3, 4))[None]
    s = jnp.where(cvalid[:, None, :][None, None, None], s, -jnp.inf)
    sh = s.shape
    p = jax.nn.softmax(s.reshape(sh[:4] + (kh * GRID_W,)), axis=-1).reshape(sh)
    o = jnp.einsum('brhqkw,brkwhd->brqhd', p.astype(v.dtype), vg)
    return o.reshape(B, S, B_W)


def _memory_attn(q, k, v):
    B, S = q.shape[0], q.shape[1]
    s = jnp.einsum('bshd,bmhd->bhsm', q, k, preferred_element_type=jnp.float32) * (HEAD_DIM ** -0.5)
    p = jax.nn.softmax(s, axis=-1)
    o = jnp.einsum('bhsm,bmhd->bshd', p.astype(v.dtype), v)
    return o.reshape(B, S, M_W)


def _trunk(x, mem, g_mix, w_in, qk_gain, sink, rpb, o_gain, w_out, g_mem, w_mem_kv, g_ff, w_ff1, w_ff2):
    B, S, _ = x.shape
    for l in range(DEPTH):
        h = _rmsnorm(x, g_mix[l])
        proj = h @ w_in[l]
        qa, ka, va, qb, kb, vb, qm = jnp.split(proj, SPLITS, axis=-1)
        qa = _rmsnorm(qa.reshape(B, S, A_Q_HEADS, HEAD_DIM), qk_gain[l, 0])
        ka = _rmsnorm(ka.reshape(B, S, A_KV_HEADS, HEAD_DIM), qk_gain[l, 1])
        va = va.reshape(B, S, A_KV_HEADS, HEAD_DIM)
        oa = _window_gqa(qa, ka, va, sink[l])
        qb = _rmsnorm(qb.reshape(B, S, B_HEADS, HEAD_DIM), qk_gain[l, 2])
        kb = _rmsnorm(kb.reshape(B, S, B_HEADS, HEAD_DIM), qk_gain[l, 3])
        vb = vb.reshape(B, S, B_HEADS, HEAD_DIM)
        ob = _neighborhood_attn(qb, kb, vb, rpb[l])
        mkv = _rmsnorm(mem, g_mem[l]) @ w_mem_kv[l]
        km, vm = jnp.split(mkv, 2, axis=-1)
        km = _rmsnorm(km.reshape(B, N_MEM, M_HEADS, HEAD_DIM), qk_gain[l, 5])
        vm = vm.reshape(B, N_MEM, M_HEADS, HEAD_DIM)
        qm = _rmsnorm(qm.reshape(B, S, M_HEADS, HEAD_DIM), qk_gain[l, 4])
        om = _memory_attn(qm, km, vm)
        og = o_gain[l]
        o = jnp.concatenate([_rmsnorm(oa, og[:A_Q]),
                             _rmsnorm(ob, og[A_Q:A_Q + B_W]),
                             _rmsnorm(om, og[A_Q + B_W:])], axis=-1)
        x = x + o @ w_out[l]
        f = _rmsnorm(x, g_ff[l]) @ w_ff1[l]
        x = x + jnp.square(jax.nn.relu(f)) @ w_ff2[l]
    return x


def setup_inputs(seed: int = 0) -> dict:
    key = jax.random.key(seed)
    ks = jax.random.split(key, 16)
    f32 = jnp.float32
    nrm = lambda k, shape: jax.random.normal(k, shape, dtype=f32)
    return {
        "x_prompt": nrm(ks[0], (BATCH, SEQ, D_MODEL)),
        "x_sample": nrm(ks[1], (DEC_BATCH, DEC_SEQ, D_MODEL)),
        "mem_prompt": nrm(ks[2], (BATCH, N_MEM, D_MODEL)),
        "mem_sample": nrm(ks[3], (DEC_BATCH, N_MEM, D_MODEL)),
        "g_mix": 1.0 + 0.02 * nrm(ks[4], (DEPTH, D_MODEL)),
        "w_in": nrm(ks[5], (DEPTH, D_MODEL, IN_W)) * D_MODEL ** -0.5,
        "qk_gain": 1.0 + 0.02 * nrm(ks[6], (DEPTH, 6, HEAD_DIM)),
        "sink": 0.5 * nrm(ks[7], (DEPTH, A_Q_HEADS)),
        "rpb": 0.1 * nrm(ks[8], (DEPTH, B_HEADS, 2 * NB_ROWS_MAX - 1, 2 * NB_COLS - 1)),
        "o_gain": 1.0 + 0.02 * nrm(ks[9], (DEPTH, MIX_W)),
        "w_out": nrm(ks[10], (DEPTH, MIX_W, D_MODEL)) * MIX_W ** -0.5,
        "g_mem": 1.0 + 0.02 * nrm(ks[11], (DEPTH, D_MODEL)),
        "w_mem_kv": nrm(ks[12], (DEPTH, D_MODEL, 2 * M_W)) * D_MODEL ** -0.5,
        "g_ff": 1.0 + 0.02 * nrm(ks[13], (DEPTH, D_MODEL)),
        "w_ff1": nrm(ks[14], (DEPTH, D_MODEL, D_FF)) * D_MODEL ** -0.5,
        "w_ff2": nrm(ks[15], (DEPTH, D_FF, D_MODEL)) * D_FF ** -0.5,
    }


def reference(x_prompt, x_sample, mem_prompt, mem_sample, g_mix, w_in, qk_gain, sink, rpb,
              o_gain, w_out, g_mem, w_mem_kv, g_ff, w_ff1, w_ff2):
    y_prompt = _trunk(x_prompt, mem_prompt, g_mix, w_in, qk_gain, sink, rpb, o_gain, w_out,
                      g_mem, w_mem_kv, g_ff, w_ff1, w_ff2)
    y_sample = _trunk(x_sample, mem_sample, g_mix, w_in, qk_gain, sink, rpb, o_gain, w_out,
                      g_mem, w_mem_kv, g_ff, w_ff1, w_ff2)
    return (y_prompt, y_sample)
```

```python
import numpy as np
from contextlib import ExitStack
import ml_dtypes
import concourse.bass as bass
import concourse.mybir as mybir
from concourse.bass_utils import run_bass_kernel_spmd

F32 = mybir.dt.float32
BF16 = mybir.dt.bfloat16
AF = mybir.ActivationFunctionType
ALU = mybir.AluOpType
AX = mybir.AxisListType

D = 1024
INW = 1792
DFF = 4096
NMEM = 256
EPS = 1e-6
NCORES = 8
EPOCH = 24000

NV = 9
NMK = 3
NB_VSPEC = [(j + 3, 0) for j in range(-3, 4)] + [(1, 1), (5, 2)]


def nb_variant(case, j):
    if case == "int" and j == -2:
        return 7
    if case == "int" and j == 2:
        return 8
    return j + 3


class _Op:
    __slots__ = ("eng", "fn", "reads", "writes", "lane", "lane_val", "deps", "signal", "sig_idx", "lag")

    def __init__(self, eng, fn, reads, writes, lane):
        self.eng, self.fn, self.reads, self.writes, self.lane = eng, fn, reads, writes, lane
        self.lane_val = None
        self.deps = ()
        self.signal = False
        self.sig_idx = None
        self.lag = 0


class Prog:
    ENGS = ("pe", "act", "dve", "pool", "sp")

    def __init__(self):
        self.ops = []

    def add(self, eng, fn, reads=(), writes=(), lane=None):
        self.ops.append(_Op(eng, fn, tuple(reads), tuple(writes), lane))

    def analyze(self):
        last_writer, readers, lane_last, lane_cnt = {}, {}, {}, {}
        for i, op in enumerate(self.ops):
            deps = set()
            for r in op.reads:
                j = last_writer.get(r)
                if j is not None:
                    deps.add(j)
                if isinstance(r, tuple) and r[0] in ("bk", "bkT"):
                    for j in readers.get(r, ()):
                        if self.ops[j].eng != op.eng:
                            deps.add(j)
            for w in op.writes:
                j = last_writer.get(w)
                if j is not None:
                    deps.add(j)
                deps.update(readers.get(w, ()))
            if op.lane is not None:
                j = lane_last.get(op.lane)
                if j is not None:
                    deps.add(j)
                lane_last[op.lane] = i
                lane_cnt[op.lane] = lane_cnt.get(op.lane, 0) + 1
                op.lane_val = 16 * lane_cnt[op.lane]
            deps.discard(i)
            real = []
            for j in deps:
                oj = self.ops[j]
                if oj.lane is None and oj.eng == "pe" and op.eng == "pe" and op.lane is None:
                    continue
                real.append(j)
                if oj.lane is None:
                    oj.signal = True
            op.deps = real
            for r in op.reads:
                readers.setdefault(r, []).append(i)
            for w in op.writes:
                last_writer[w] = i
                readers[w] = []
        cnt = {e: 0 for e in self.ENGS}
        for op in self.ops:
            if op.signal:
                op.sig_idx = cnt[op.eng]
                cnt[op.eng] += 1
        self.sig_counts = cnt
        self.lanes = list(lane_cnt.keys())

    def emit(self, nc, es):
        self.analyze()
        sems = {}
        for e in self.ENGS:
            for ep in range(self.sig_counts[e] // EPOCH + 1):
                sems[("eng", e, ep)] = es.enter_context(nc.semaphore(f"s_{e}_{ep}"))
        for k, lane in enumerate(self.lanes):
            sems[("lane", lane)] = es.enter_context(nc.semaphore(f"l_{k}"))
        block = es.enter_context(nc.Block())
        per_eng = {e: [op for op in self.ops if op.eng == e] for e in self.ENGS}
        ops = self.ops

        def run(engname, e):
            waited = {}
            for op in per_eng[engname]:
                need = {}
                for j in op.deps:
                    oj = ops[j]
                    if oj.lane is not None:
                        key = ("lane", oj.lane)
                        val = oj.lane_val
                    else:
                        key = ("eng", oj.eng)
                        val = oj.sig_idx + 1
                    if need.get(key, 0) < val:
                        need[key] = val
                for key, val in need.items():
                    if waited.get(key, 0) >= val:
                        continue
                    waited[key] = val
                    if key[0] == "lane":
                        e.wait_ge(sems[key], val)
                    else:
                        ep, loc = divmod(val - 1, EPOCH)
                        e.wait_ge(sems[("eng", key[1], ep)], loc + 1)
                ins = op.fn(e)
                if op.lane is not None:
                    ins.then_inc(sems[("lane", op.lane)], 16)
                elif op.signal:
                    ep = op.sig_idx // EPOCH
                    ins.then_inc(sems[("eng", engname, ep)], 1)

        @block.sync
        def _(e):
            run("sp", e)

        @block.gpsimd
        def _(e):
            run("pool", e)

        @block.scalar
        def _(e):
            run("act", e)

        @block.vector
        def _(e):
            run("dve", e)

        @block.tensor
        def _(e):
            run("pe", e)


class _Rot:
    def __init__(self, n):
        self.n, self.i = n, -1

    def next(self):
        self.i = (self.i + 1) % self.n
        return self.i


class Builder:
    def __init__(self, segs, depth=2):
        self.segs = segs
        self.depth = depth
        self.nc = bass.Bass("TRN2", target_bir_lowering=False)
        self.P = Prog()
        self.lane = 0

    def op(self, eng, fn, reads=(), writes=(), lane=None, lag=0):
        self.P.add(eng, fn, reads, writes, lane)
        self.P.ops[-1].lag = lag

    def streams(self, fns):
        lists = []
        lane0 = self.lane
        for k, fn in enumerate(fns):
            saved = self.P.ops
            self.P.ops = []
            self.lane = k
            fn()
            lists.append(self.P.ops)
            self.P.ops = saved
        self.lane = lane0
        idx = [0] * len(lists)
        alive = True
        while alive:
            alive = False
            for k, lst in enumerate(lists):
                if idx[k] < len(lst):
                    self.P.ops.append(lst[idx[k]])
                    idx[k] += 1
                    alive = True

    def merge_prop(self, f_main, f_fill):
        lists = []
        for fn in (f_main, f_fill):
            saved = self.P.ops
            self.P.ops = []
            fn()
            lists.append(self.P.ops)
            self.P.ops = saved
        M, Fl = lists
        out = self.P.ops
        ix = 0
        for iy, y in enumerate(M):
            if y.lag:
                need = y.lag
                while need > 0 and ix < len(Fl):
                    x = Fl[ix]
                    out.append(x)
                    ix += 1
                    if x.eng == "pe":
                        need -= 1
            out.append(y)
        out.extend(Fl[ix:])

    def dma(self, eng, out, in_, reads, writes, lane, **kw):
        self.op(eng, lambda e, out=out, in_=in_, kw=kw: e.dma_start(out=out, in_=in_, **kw),
                reads, writes, lane)

    def build(self):
        nc = self.nc
        L = self.depth
        with ExitStack() as es:
            self.es = es
            self._declare_dram()
            self._alloc()
            self._startup()
            for si, (name, S) in enumerate(self.segs):
                for l in range(L):
                    src = self.x_in[si] if l == 0 else self.x_mid[si][(l - 1) % 2]
                    dst = self.y_out[si] if l == L - 1 else self.x_mid[si][l % 2]
                    srck = ("din", si) if l == 0 else ("dmid", si, (l - 1) % 2)
                    dstk = ("dout", si) if l == L - 1 else ("dmid", si, l % 2)
                    self._layer_pass(si, S, l, src, dst, srck, dstk)
            fin_reads = []
            for si, (name, S) in enumerate(self.segs):
                fin_reads += [("dout", si, c) for c in range(S // 512)]
            self.op("pool", lambda e: e.nop(), reads=fin_reads)
            self.P.emit(nc, es)
        return nc

    def _declare_dram(self):
        nc, L = self.nc, self.depth
        dt = nc.dram_tensor
        self.x_in, self.y_out, self.x_mid, self.mem_in = [], [], [], []
        for si, (name, S) in enumerate(self.segs):
            self.x_in.append(dt(f"x_{name}", [S, D], F32, kind="ExternalInput").ap())
            self.mem_in.append(dt(f"mem_{name}", [NMEM, D], F32, kind="ExternalInput").ap())
            self.y_out.append(dt(f"y_{name}", [S, D], F32, kind="ExternalOutput").ap())
            self.x_mid.append([dt(f"xmid_{name}_{k}", [S, D], F32, kind="Internal").ap()
                               for k in range(min(2, max(L - 1, 1)))])
        self.w_in = dt("w_in", [L, D, INW], F32, kind="ExternalInput").ap()
        self.w_out = dt("w_out", [L, D, D], F32, kind="ExternalInput").ap()
        self.w_mem = dt("w_mem_kv", [L, D, 512], F32, kind="ExternalInput").ap()
        self.w_ff1 = dt("w_ff1", [L, D, DFF], F32, kind="ExternalInput").ap()
        self.w_ff2 = dt("w_ff2", [L, DFF, D], F32, kind="ExternalInput").ap()
        self.gcols_d = dt("gcols", [128, L * 4 * 8], F32, kind="ExternalInput").ap()
        self.gaincols_d = dt("gaincols", [128, L * 6], F32, kind="ExternalInput").ap()
        self.gainrows_d = dt("gainrows", [1, L * 6 * 64], F32, kind="ExternalInput").ap()
        self.sink_d = dt("sinkrow", [1, L * 8], F32, kind="ExternalInput").ap()
        self.biasT_d = dt("biasT", [L, 128, 4 * 7 * 128], F32, kind="ExternalInput").ap()
        self.ident_d = dt("ident", [128, 128], BF16, kind="ExternalInput").ap()
        self.ea_d = dt("ea", [128, 6 * 512], BF16, kind="ExternalInput").ap()
        self.nmask_d = dt("nmask", [128, NMK * 128], BF16, kind="ExternalInput").ap()
        self.win_s = dt("win_s", [L, 4, 128, 8, 512], BF16, kind="Internal").ap()
        self.wout_s = dt("wout_s", [L, 2, 128, 8, 512], BF16, kind="Internal").ap()
        self.wm_s = dt("wm_s", [L, 128, 8, 512], BF16, kind="Internal").ap()
        self.w1_s = dt("w1_s", [L, 128, 8, 4, 8, 128], BF16, kind="Internal").ap()
        self.w2_s = dt("w2_s", [L, 2, 4, 128, 8, 512], BF16, kind="Internal").ap()

    def _alloc(self):
        nc, es, L = self.nc, self.es, self.depth

        def sb(name, shape, dtype):
            return es.enter_context(nc.sbuf_tensor(name, shape, dtype))

        self.ident = sb("ident_sb", [128, 128], BF16)
        self.ea = sb("ea_sb", [128, 6, 512], BF16)
        self.enb = sb("enb", [128, NV, 4, 128], BF16)
        self.gcols = sb("gcols_sb", [128, L * 4 * 8], F32)
        self.gaincols = sb("gaincols_sb", [128, L * 6], F32)
        self.gainrows = sb("gainrows_sb", [128, L * 6, 64], F32)
        self.gprod = sb("gprod", [128, L * 3, 64], F32)
        self.gs = sb("gs", [128, L * 3], F32)
        self.negB = sb("negB", [128, L * 3], F32)
        self.sinkexp = sb("sinkexp", [128, L * 8], F32)
        self.epsc = sb("epsc", [128, 1], F32)
        self.xbuf = sb("xbuf", [128, 2, 4, D], F32)
        self.xst = sb("xst", [128, 2, D], F32)
        self.xsb = sb("xsb", [128, 3, D], BF16)
        self.hT = sb("hT", [128, 8, 512], BF16)
        self.oxT = sb("oxT", [128, 8, 512], BF16)
        self.QT = sb("QT", [128, 2, 4, 8, 128], BF16)
        self.KT = sb("KT", [128, 3, 3, 512], BF16)
        self.Vb = sb("Vb", [128, 3, 4, 6, 65], BF16)
        self.KmT = sb("KmT", [128, 2, 256], BF16)
        self.Vm = sb("Vm", [128, 2, 4, 65], BF16)
        self.praw = sb("praw", [128, 2, 512], BF16)
        self.qkn = sb("qkn", [128, 2, 512], BF16)
        self.sqb = sb("sqb", [128, 2, 512], BF16)
        self.ssq = sb("ssq", [128, 2, 8], F32)
        self.rsq = sb("rsq", [128, 2, 8], F32)
        self.pexp = sb("pexp", [128, 4, 512], BF16)
        self.of32 = sb("of32", [128, 2, D], F32)
        self.estage = self.of32[:, 0, 0:896].rearrange("p (j n) -> p j n", j=7)
        self.obf = sb("obf", [128, 2, D], BF16)
        self.dtmp = sb("dtmp", [128, 2, 4, 4], F32)
        self.sso = sb("sso", [128, 2, 4], F32)
        self.ssx = sb("ssx", [128, 3, 1], F32)
        self.gT = sb("gT", [128, 32, 512], BF16)
        self.nmask = sb("nmask_sb", [128, NMK, 128], BF16)
        self.rtmp = sb("rtmp", [128, 2, 512], BF16)
        self.wslab = sb("wslab", [128, 4, 4096], BF16)
        self.bankF = [es.enter_context(nc.psum_tensor(f"bF{k}", [128, 512], F32)) for k in range(6)]
        self.bankT = [es.enter_context(nc.psum_tensor(f"bT{k}", [128, 1024], BF16)) for k in range(2)]
        self.rSl = [_Rot(2), _Rot(2)]
        self.rO = _Rot(3)
        self.rOP = _Rot(2)
        self.rF1 = _Rot(2)
        self.rX = _Rot(2)
        self.rpe4 = _Rot(4)
        self.rpu = _Rot(2)
        self.rPM = _Rot(2)
        self.pu_pending = None
        self.rS4 = _Rot(4)
        self.rrt = _Rot(2)
        self.rwsY, self.rwsX = _Rot(2), _Rot(2)
        self.flip = 0

    def bank_S(self):
        k = self.rS4.next()
        return self.bankF[k], ("bk", k)

    def bank_M(self):
        k = self.rPM.next()
        return self.bankF[k], ("bk", k)

    def bank_X(self):
        k = 2 + self.rX.next()
        return self.bankF[k], ("bk", k)

    def bank_M3(self):
        k = self.rO.next()
        return self.bankF[k], ("bk", k)

    def bank_O(self):
        k = 4 + self.lane
        return self.bankF[k], ("bk", k)

    def bank_T(self):
        k = self.lane
        if k == 2:
            return self.bankF[4][:].bitcast(BF16), ("bk", 4)
        return self.bankT[k], ("bkT", k)

    def _startup(self):
        L = self.depth
        op, dma = self.op, self.dma
        dma("pool", self.ident[:], self.ident_d, [], ["ident"], "c0")
        dma("pool", self.ea[:], self.ea_d.rearrange("p (a n) -> p a n", a=6), [], ["ea"], "c1")
        dma("pool", self.nmask[:], self.nmask_d.rearrange("p (v n) -> p v n", v=NMK), [], ["nmask"], "c2")
        dma("pool", self.gcols[:], self.gcols_d, [], ["gcols"], "c3")
        dma("pool", self.gaincols[:], self.gaincols_d, [], ["gaincols"], "c4")
        dma("pool", self.gainrows[:].rearrange("p a d -> p (a d)"),
            self.gainrows_d[0].partition_broadcast(128), [], ["gainrows"], "c5")
        dma("pool", self.sinkexp[:], self.sink_d[0].partition_broadcast(128), [], ["sinkraw"], "c6")
        op("dve", lambda e: e.memset(self.epsc[:], EPS), [], ["epsc"])
        op("dve", lambda e: e.memset(self.Vb[:, :, :, :, 64:65], 1.0), [],
           [("V", s, t) for s in range(3) for t in range(4)])
        op("dve", lambda e: e.memset(self.Vm[:, :, :, 64:65], 1.0), [], ["Vm"])
        for l in range(L):
            for ty in range(3):
                c = l * 3 + ty
                a, b = l * 6 + 2 * ty, l * 6 + 2 * ty + 1
                op("dve", lambda e, c=c, a=a, b=b: e.scalar_tensor_tensor(
                    out=self.gs[:, c:c + 1], in0=self.gaincols[:, a:a + 1], scalar=0.125,
                    in1=self.gaincols[:, b:b + 1], op0=ALU.mult, op1=ALU.mult),
                   ["gaincols"], ["gs"])
                op("dve", lambda e, c=c, a=a, b=b: e.tensor_tensor(
                    out=self.gprod[:, c, :], in0=self.gainrows[:, a, :], in1=self.gainrows[:, b, :],
                    op=ALU.mult), ["gainrows"], [("gprod", c)])
                op("dve", lambda e, c=c: e.tensor_reduce(
                    out=self.negB[:, c:c + 1], in_=self.gprod[:, c, :], axis=AX.X, op=ALU.max,
                    apply_absolute_value=True), [("gprod", c)], ["negB"])
                op("dve", lambda e, c=c: e.tensor_scalar(
                    out=self.negB[:, c:c + 1], in0=self.negB[:, c:c + 1], scalar1=-8.0, scalar2=None,
                    op0=ALU.mult), ["negB"], ["negB"])
            op("act", lambda e, l=l: e.activation(
                out=self.sinkexp[:, l * 8:(l + 1) * 8], in_=self.sinkexp[:, l * 8:(l + 1) * 8], func=AF.Exp,
                bias=self.negB[:, l * 3:l * 3 + 1], scale=1.0),
               ["sinkraw", "negB"], ["sinkexp"])
        self._prep_weights()

    def _prep_weights(self):
        L = self.depth
        op, dma = self.op, self.dma
        xb = self.xbuf
        gt = self.gT
        rr = _Rot(4)
        eng_flip = [0]
        self.wscr_keys = []

        def stage(src_ap, ncols, gcol, stores, a3=None):
            r = rr.next()
            if r < 2:
                a = xb[:, r].rearrange("p t d -> p (t d)")[:, 0:ncols]
                xk = [("xb", r, t) for t in range(4)]
            else:
                q = 2 * (r - 2)
                a = self.wslab[:, q:q + 2, :].rearrange("p s n -> p (s n)").bitcast(F32)[:, 0:ncols]
                xk = [("ws", q), ("ws", q + 1)]
            a_dst = a if a3 is None else a3(a)
            b = gt[:, r * 8:(r + 1) * 8, :].rearrange("p j n -> p (j n)")[:, 0:ncols]
            gk = [("gT", j) for j in range(r * 8, (r + 1) * 8)]
            dma("sp", a_dst, src_ap, [], xk, ("pl", r))
            eng = "dve" if eng_flip[0] % 2 == 0 else "act"
            eng_flip[0] += 1
            if eng == "dve":
                if gcol is None:
                    op("dve", lambda e, a=a, b=b: e.tensor_copy(out=b, in_=a), xk, gk)
                else:
                    op("dve", lambda e, a=a, b=b, g=gcol: e.tensor_scalar(
                        out=b, in0=a, scalar1=g, scalar2=None, op0=ALU.mult), xk + ["gcols"], gk)
            else:
                if gcol is None:
                    op("act", lambda e, a=a, b=b: e.activation(out=b, in_=a, func=AF.Copy), xk, gk)
                else:
                    op("act", lambda e, a=a, b=b, g=gcol: e.activation(
                        out=b, in_=a, func=AF.Copy, scale=g), xk + ["gcols"], gk)
            for n, (dst, srcview) in enumerate(stores):
                wk_ = ("wscr", len(self.wscr_keys))
                self.wscr_keys.append(wk_)
                dma("sp", dst, srcview(b), gk, [wk_], ("ps", r, n))

        for l in range(L):
            def gc(v, k, l=l):
                c = (l * 4 + v) * 8 + k
                return self.gcols[:, c:c + 1]
            for k in range(8):
                rows = slice(k * 128, (k + 1) * 128)
                stage(self.w_in[l, rows, :], INW, gc(0, k), [
                    (self.win_s[l, 0:3, :, k, :].rearrange("s p n -> p s n"),
                     lambda b: b[:, 0:1536].rearrange("p (s n) -> p s n", s=3)),
                    (self.win_s[l, 3, :, k, 0:256], lambda b: b[:, 1536:1792]),
                ])
                stage(self.w_out[l, rows, :], D, gc(1, k), [
                    (self.wout_s[l, :, :, k, :].rearrange("s p n -> p s n"),
                     lambda b: b[:, 0:1024].rearrange("p (s n) -> p s n", s=2)),
                ])
                stage(self.w_mem[l, rows, :], 512, gc(2, k), [
                    (self.wm_s[l, :, k, :], lambda b: b[:, 0:512]),
                ])
                stage(self.w_ff1[l, rows, :], DFF, gc(3, k), [
                    (self.w1_s[l, :, :, :, k, :].rearrange("p s j h -> p (s j) h"),
                     lambda b: b[:, 0:4096].rearrange("p (sj h) -> p sj h", sj=32)),
                ])
            for jg in range(8):
                ns, jj0 = jg // 2, (jg % 2) * 4
                src = self.w_ff2[l, jg * 512:(jg + 1) * 512, :].rearrange("(j p) c -> p j c", p=128)
                stage(src, 4096, None, [
                    (self.w2_s[l, n, ns, :, jj0:jj0 + 4, :],
                     (lambda b, n=n: b[:, 0:4096].rearrange("p (j c) -> p j c", j=4)[:, :, n * 512:(n + 1) * 512]))
                    for n in range(2)
                ], a3=lambda a: a.rearrange("p (j c) -> p j c", j=4))

    def slab_slot(self, ring):
        return self.rwsY.next() if ring == "Y" else 2 + self.rwsX.next()

    def load_slab(self, src_ap, ncols, ring="Y"):
        s = self.slab_slot(ring)
        dst = self.wslab[:, s, 0:ncols]
        self.dma("sp", dst, src_ap, self.wscr_keys, [("ws", s)], ("ws", s))
        return s

    def rstd_chain(self, ss_ap, out_ap, n, rk, wk):
        self.op("act", lambda e: e.activation(out=out_ap, in_=ss_ap, func=AF.Ln, scale=1.0 / n,
                                              bias=self.epsc[:, 0:1]), list(rk) + ["epsc"], list(wk))
        self.op("act", lambda e: e.activation(out=out_ap, in_=out_ap, func=AF.Exp, scale=-0.5),
                list(wk), list(wk))

    def norm_pipe(self, items):
        pend = None
        for k, (src_ap, src_keys, dstT_ap, dst_keys, pre) in enumerate(items):
            if pre is not None:
                pre()
            back = self.norm_transpose(src_ap, src_keys, dstT_ap, dst_keys, slot=k % 3, tbank=k % 2, defer=True)
            if pend is not None:
                pend()
            pend = back
        if pend is not None:
            pend()

    def norm_transpose(self, src_ap, src_keys, dstT_ap, dst_keys, slot=None, tbank=None, defer=False):
        r = self.lane if slot is None else slot
        ss = self.ssx[:, r, :]
        xr = r
        xs = self.xsb[:, xr, :]
        self.op("act", lambda e: e.activation(out=xs, in_=src_ap, func=AF.Square, accum_out=ss),
                src_keys, [("ssx", r), ("xsb", xr)])
        self.rstd_chain(ss, ss, D, [("ssx", r)], [("ssx", r)])
        self.op("dve", lambda e: e.tensor_scalar(out=xs, in0=src_ap, scalar1=ss, scalar2=None, op0=ALU.mult),
                list(src_keys) + [("ssx", r)], [("xsb", xr)])
        if tbank is None:
            bT, bk = self.bank_T()
        else:
            bT, bk = self.bankT[tbank], ("bkT", tbank)

        def tr(e):
            for k in range(8):
                ins = e.transpose(out=bT[:, k * 128:(k + 1) * 128], in_=xs[:, k * 128:(k + 1) * 128],
                                  identity=self.ident[:])
            return ins

        def back():
            self.op("pe", tr, [("xsb", xr), "ident"], [bk], lag=1)
            self.op("act", lambda e: e.activation(out=dstT_ap, in_=bT[:].rearrange("p (k n) -> p k n", k=8),
                                                  func=AF.Copy), [bk], dst_keys)
        if defer:
            return back
        back()

    def proj_unit(self, bank, bkey, norms, vcopies, pair=False):
        for (c0, nh, dst, dk) in vcopies:
            self.op("act", lambda e, c0=c0, nh=nh, dst=dst: e.activation(
                out=dst, in_=bank[:, c0:c0 + nh * 64].rearrange("p (h d) -> p h d", h=nh), func=AF.Copy),
                [bkey], [dk])
        for (c0, nh, blocks) in norms:
            r = self.rpu.next()
            w = nh * 64
            praw = self.praw[:, r, 0:w]
            sq = self.sqb[:, r, 0:w]
            qkn = self.qkn[:, r, 0:w]
            ssq = self.ssq[:, r, 0:nh]
            rsq = self.rsq[:, r, 0:nh]
            self.op("act", lambda e, c0=c0, w=w, praw=praw: e.activation(
                out=praw, in_=bank[:, c0:c0 + w], func=AF.Copy), [bkey], [("praw", r)])
            self.op("dve", lambda e, praw=praw, sq=sq: e.tensor_tensor(out=sq, in0=praw, in1=praw, op=ALU.mult),
                    [("praw", r)], [("sqb", r)])
            self.op("dve", lambda e, sq=sq, ssq=ssq, nh=nh: e.tensor_reduce(
                out=ssq, in_=sq.rearrange("p (h d) -> p h d", h=nh), axis=AX.X, op=ALU.add),
                [("sqb", r)], [("ssq", r)])
            self.rstd_chain(ssq, rsq, 64, [("ssq", r)], [("rsq", r)])
            if pair:
                self.op("dve", lambda e, praw=praw, qkn=qkn, rsq=rsq: e.tensor_tensor(
                    out=qkn.rearrange("p (b g d) -> p g b d", b=4, g=2),
                    in0=praw.rearrange("p (g b d) -> p g b d", g=2, b=4),
                    in1=rsq.rearrange("p (g b) -> p g b", g=2).unsqueeze(3).to_broadcast([128, 2, 4, 64]),
                    op=ALU.mult), [("praw", r), ("rsq", r)], [("qkn", r)])
            else:
                self.op("dve", lambda e, praw=praw, qkn=qkn, rsq=rsq, nh=nh: e.tensor_tensor(
                    out=qkn.rearrange("p (h d) -> p h d", h=nh), in0=praw.rearrange("p (h d) -> p h d", h=nh),
                    in1=rsq.unsqueeze(2).to_broadcast([128, nh, 64]), op=ALU.mult),
                    [("praw", r), ("rsq", r)], [("qkn", r)])
            bT, bk = self.bankT[r], ("bkT", r)
            q3 = qkn.rearrange("p (h d) -> p h d", h=nh)

            def finish(blocks=blocks, q3=q3, bT=bT, bk=bk, r=r):
                def tr(e):
                    for bi, (hs, dst, dk, sc) in enumerate(blocks):
                        ins = e.transpose(out=bT[:, bi * 128:(bi + 1) * 128],
                                          in_=q3[:, hs, :].rearrange("p h d -> p (h d)"), identity=self.ident[:])
                    return ins
                self.op("pe", tr, [("qkn", r), "ident"], [bk], lag=3)
                for bi, (hs, dst, dk, sc) in enumerate(blocks):
                    src = bT[:, bi * 128:(bi + 1) * 128]
                    if sc is None:
                        self.op("dve", lambda e, dst=dst, src=src: e.tensor_copy(out=dst, in_=src), [bk], [dk])
                    else:
                        self.op("act", lambda e, dst=dst, src=src, sc=sc: e.activation(
                            out=dst, in_=src, func=AF.Copy, scale=sc), [bk, "gs"], [dk])
            self.pu_flush()
            self.pu_pending = finish

    def pu_flush(self):
        if self.pu_pending is not None:
            f = self.pu_pending
            self.pu_pending = None
            f()

    def attn_unit(self, smm, skeys, negB_ap, emask, ekey, pvs, first, last):
        r = self.rpe4.next()
        pe_ = self.pexp[:, r, :]
        if pvs == "split":
            bSa, bSak = self.bank_S()
            bSb, bSbk = self.bank_S()
            self.op("pe", lambda e: smm(e, (bSa, bSb)), skeys, [bSak, bSbk])
            self.op("act", lambda e: e.activation(out=pe_[:, 0:256], in_=bSa[:, 0:256], func=AF.Exp,
                                                  bias=negB_ap, scale=1.0), [bSak, "negB"], [("pexp", r)])
            self.op("act", lambda e: e.activation(out=pe_[:, 256:512], in_=bSb[:, 0:256], func=AF.Exp,
                                                  bias=negB_ap, scale=1.0), [bSbk, "negB", ("pexp", r)],
                    [("pexp", r)])
        else:
            bS, bSk = self.bank_S()
            self.op("pe", lambda e: smm(e, bS), skeys, [bSk])
            self.op("act", lambda e: e.activation(out=pe_, in_=bS[:], func=AF.Exp, bias=negB_ap, scale=1.0),
                    [bSk, "negB"], [("pexp", r)])
        if emask is not None:
            self.op("dve", lambda e: e.tensor_tensor(out=pe_, in0=pe_, in1=emask, op=ALU.mult),
                    [("pexp", r), ekey], [("pexp", r)])
        return pe_, ("pexp", r)

    def build_enb(self, l):
        for h in range(4):
            ek = [("of32", 0, q) for q in range(4)]
            self.dma("pool", self.of32[:, 0, 0:896],
                     self.biasT_d[l, :, h * 896:(h + 1) * 896], [], ek, "est")
            self.op("act", lambda e: e.activation(out=self.of32[:, 0, 0:896], in_=self.of32[:, 0, 0:896],
                                                  func=AF.Exp), ek, ek)
            for v, (ji, mi) in enumerate(NB_VSPEC):
                self.op("dve", lambda e, v=v, ji=ji, mi=mi, h=h: e.tensor_tensor(
                    out=self.enb[:, v, (h % 2) * 2 + h // 2, :], in0=self.estage[:, ji, :],
                    in1=self.nmask[:, mi, :],
                    op=ALU.mult), ek + ["nmask"], ["enb"])

    def mem_kv(self, si, l):
        memd = self.mem_in[si]

        def nt_(mt):
            xk = [("xst", mt)]
            self.dma("pool", self.xst[:, mt, :], memd[mt * 128:(mt + 1) * 128, :], [], xk, ("xst", mt))
            self.norm_transpose(self.xst[:, mt, :], xk, self.hT[:, :, mt * 128:(mt + 1) * 128], [("hT", mt)])
        self.streams([lambda: nt_(0), lambda: nt_(1)])
        s = self.load_slab(self.wm_s[l].rearrange("p k n -> p (k n)"), 4096)
        wsl = self.wslab[:, s, :].rearrange("p (k n) -> p k n", k=8)
        gsm = self.gs[:, l * 3 + 2:l * 3 + 3]

        def pj(mt):
            bM, bMk = self.bank_M()

            def mm(e, mt=mt, bM=bM):
                for k in range(8):
                    ins = e.matmul(bM[:], lhsT=self.hT[:, k, mt * 128:(mt + 1) * 128], rhs=wsl[:, k, :],
                                   start=(k == 0), stop=(k == 7))
                return ins
            self.op("pe", mm, [("hT", mt), ("ws", s)], [bMk])
            blocks = [(slice(2 * b, 2 * b + 2), self.KmT[:, b, mt * 128:(mt + 1) * 128], "KmT", gsm)
                      for b in range(2)]
            self.proj_unit(bM, bMk, [(0, 4, blocks)], [(256, 4, self.Vm[:, mt, :, 0:64], "Vm")])
        pj(0)
        pj(1)
        self.pu_flush()

    def _layer_pass(self, si, S, l, src, dst, srck, dstk):
        nch = S // 512
        self.build_enb(l)
        self.mem_kv(si, l)
        for c in range(min(3, nch)):
            if c == 2:
                self.A_att(si, S, l, 0, src, srck)
                self.A_xnorm(0)
            self.P_stage(si, S, l, c, src, srck)
        if nch <= 2:
            self.A_att(si, S, l, 0, src, srck)
            self.A_xnorm(0)
        for i in range(nch):
            if i + 1 < nch:
                self.merge_prop(lambda: self.A_att(si, S, l, i + 1, src, srck), lambda: self.A_ff1(l, i))
            else:
                self.A_ff1(l, i)

            def y2(i=i):
                pre = None
                if i + 3 < nch:
                    pre = {sl: self.P_slab(l, sl) for sl in range(2)}
                if i + 1 < nch:
                    self.A_xnorm(i + 1)
                if i + 3 < nch:
                    self.P_stage(si, S, l, i + 3, src, srck, pre)
            self.merge_prop(y2, lambda: self.A_ff2(l, i, dst, dstk))

    def P_slab(self, l, sl):
        if sl < 3:
            s = self.load_slab(self.win_s[l, sl].rearrange("p k n -> p (k n)"), 4096)
            wsl = self.wslab[:, s, :].rearrange("p (k n) -> p k n", k=8)
        else:
            s = self.slab_slot("Y")
            wsl = self.wslab[:, s, 0:2048].rearrange("p (k n) -> p k n", k=8)
            self.dma("sp", wsl, self.win_s[l, 3, :, :, 0:256], self.wscr_keys, [("ws", s)], ("ws", s))
        return s, wsl

    def P_stage(self, si, S, l, i, src, srck, pre=None):
        qs, ks = i % 2, i % 3
        slabs = dict(pre) if pre else {}
        for sl in range(2):
            if sl not in slabs:
                slabs[sl] = self.P_slab(l, sl)

        items = []
        for t in range(4):
            ln = t % 2

            def pre(t=t, ln=ln):
                self.dma("pool", self.xst[:, ln, :], src[i * 512 + t * 128:i * 512 + (t + 1) * 128, :],
                         [srck + (i,)], [("xst", ln)], ("xst", ln))
            items.append((self.xst[:, ln, :], [("xst", ln)], self.hT[:, :, t * 128:(t + 1) * 128],
                          [("hT", t)], pre))
        self.norm_pipe(items)
        gsa = self.gs[:, l * 3 + 0:l * 3 + 1]
        gsb = self.gs[:, l * 3 + 1:l * 3 + 2]
        for sl in range(4):
            ncol = 512 if sl < 3 else 256
            if sl not in slabs:
                slabs[sl] = self.P_slab(l, sl)
            s, wsl = slabs[sl]

            def pj(t, sl=sl, s=s, wsl=wsl, ncol=ncol):
                bM, bMk = self.bank_M()
                tc = slice(t * 128, (t + 1) * 128)

                def mm(e):
                    for k in range(8):
                        ins = e.matmul(bM[:, 0:ncol], lhsT=self.hT[:, k, tc], rhs=wsl[:, k, :],
                                       start=(k == 0), stop=(k == 7))
                    return ins
                self.op("pe", mm, [("hT", t), ("ws", s)], [bMk])
                QTk, KTk, Vk = ("QT", qs, t), ("KT", ks, t), ("V", ks, t)
                if sl == 0:
                    blocks = [(slice(2 * b, 2 * b + 2), self.QT[:, qs, t, b, :], QTk, None) for b in range(4)]
                    self.proj_unit(bM, bMk, [(0, 8, blocks)], [], pair=True)
                elif sl == 1:
                    kb = [(slice(0, 2), self.KT[:, ks, 0, tc], KTk, gsa)]
                    qb = [(slice(2 * b, 2 * b + 2), self.QT[:, qs, t, 4 + b, :], QTk, None) for b in range(2)]
                    self.proj_unit(bM, bMk, [(0, 2, kb), (256, 4, qb)],
                                   [(128, 2, self.Vb[:, ks, t, 0:2, 0:64], Vk)])
                elif sl == 2:
                    kb = [(slice(2 * b, 2 * b + 2), self.KT[:, ks, 1 + b, tc], KTk, gsb) for b in range(2)]
                    self.proj_unit(bM, bMk, [(0, 4, kb)], [(256, 4, self.Vb[:, ks, t, 2:6, 0:64], Vk)])
                else:
                    qm = [(slice(2 * b, 2 * b + 2), self.QT[:, qs, t, 6 + b, :], QTk, None) for b in range(2)]
                    self.proj_unit(bM, bMk, [(0, 4, qm)], [])
            for t in range(4):
                pj(t)
        self.pu_flush()

    def A_att(self, si, S, l, i, src, srck):
        xs_ = i % 2
        self.dma("pool", self.xbuf[:, xs_], src[i * 512:(i + 1) * 512, :].rearrange("(t p) d -> p t d", p=128),
                 [srck + (i,)], [("xb", xs_, t) for t in range(4)], ("xb", xs_))
        nt = S // 128
        qs = i % 2
        negBa = self.negB[:, l * 3 + 0:l * 3 + 1]
        negBb = self.negB[:, l * 3 + 1:l * 3 + 2]
        negBm = self.negB[:, l * 3 + 2:l * 3 + 3]
        units = []
        ngrp = 0
        for t in range(4):
            T = 4 * i + t
            QTk = ("QT", qs, t)
            ln = t % 2
            for g in range(2):
                ob = ngrp % 2
                ngrp += 1
                bO, bOk = self.bankT[ob][:].bitcast(F32), ("bkT", ob)
                o3 = bO[:, 0:260].rearrange("p (h c) -> p h c", h=4)
                rels = [r for r in (-1, 0, 1) if 0 <= T + r < nt]
                ps = slice(64 * g, 64 * g + 64)
                for r_ in rels:
                    U = T + r_
                    cs, tu = (U // 4) % 3, U % 4
                    uc = slice(tu * 128, (tu + 1) * 128)

                    def smm(e, bS, cs=cs, uc=uc, ps=ps, t=t):
                        return e.matmul(bS[:], lhsT=self.KT[ps, cs, 0, uc],
                                        rhs=self.QT[ps, qs, t, 0:4, :].rearrange("p b n -> p (b n)"),
                                        start=True, stop=True)
                    first, last = (r_ == rels[0]), (r_ == rels[-1])

                    def pv(e, pt, cs=cs, tu=tu, first=first, last=last, o3=o3, g=g):
                        for hh in range(4):
                            ins = e.matmul(o3[:, hh, :], lhsT=pt[:, hh * 128:(hh + 1) * 128],
                                           rhs=self.Vb[:, cs, tu, g, :], start=(first and hh == 0), stop=last,
                                           skip_group_check=True)
                        return ins
                    fin = None
                    if last:
                        fin = (bO, bOk, o3, g * 256, self.sinkexp[:, l * 8 + 4 * g:l * 8 + 4 * g + 4],
                               "sinkexp", g, ln)
                    units.append(dict(smm=smm, skeys=[("KT", cs, tu), QTk], negB=negBa,
                                      emask=self.ea[:, g * 3 + r_ + 1, :], ekey="ea", split=False,
                                      pv=pv, pvkeys=[("V", cs, tu)], bOk=bOk, fin=fin, epi=None))
            if T == 0:
                case, js = "top0", [0, 1, 2, 3]
            elif T == 1:
                case, js = "top1", [-1, 0, 1, 2]
            elif T == nt - 2:
                case, js = "bot1", [-2, -1, 0, 1]
            elif T == nt - 1:
                case, js = "bot0", [-3, -2, -1, 0]
            else:
                case, js = "int", [-2, -1, 0, 1, 2]
            ob = ngrp % 2
            ngrp += 1
            bO, bOk = self.bankT[ob][:].bitcast(F32), ("bkT", ob)
            o3 = bO[:, 0:260].rearrange("p (h c) -> p h c", h=4)
            for j in js:
                U = T + j
                cs, tu = (U // 4) % 3, U % 4
                uc = slice(tu * 128, (tu + 1) * 128)
                v = nb_variant(case, j)

                def smm(e, bS, cs=cs, uc=uc, t=t):
                    for h in range(4):
                        ps = slice(64 * (h % 2), 64 * (h % 2) + 64)
                        ins = e.matmul(bS[h % 2][:, (h // 2) * 128:(h // 2 + 1) * 128],
                                       lhsT=self.KT[ps, cs, 1 + h // 2, uc],
                                       rhs=self.QT[ps, qs, t, 4 + h // 2, :], start=True, stop=True)
                    return ins
                first, last = (j == js[0]), (j == js[-1])

                def pv(e, pt, cs=cs, tu=tu, first=first, last=last, o3=o3):
                    for h in range(4):
                        hp = (h % 2) * 2 + h // 2
                        ins = e.matmul(o3[:, h, :], lhsT=pt[:, hp * 128:(hp + 1) * 128],
                                       rhs=self.Vb[:, cs, tu, 2 + h, :], start=(first and h == 0), stop=last,
                                       skip_group_check=True)
                    return ins
                fin = (bO, bOk, o3, 512, None, None, 2, ln) if last else None
                units.append(dict(smm=smm, skeys=[("KT", cs, tu), QTk], negB=negBb,
                                  emask=self.enb[:, v].rearrange("p h n -> p (h n)"), ekey="enb", split=True,
                                  pv=pv, pvkeys=[("V", cs, tu)], bOk=bOk, fin=fin, epi=None))
            ob = ngrp % 2
            ngrp += 1
            bO, bOk = self.bankT[ob][:].bitcast(F32), ("bkT", ob)
            o3 = bO[:, 0:260].rearrange("p (h c) -> p h c", h=4)
            for mt in range(2):
                mc = slice(mt * 128, (mt + 1) * 128)

                def smm(e, bS, mc=mc, t=t):
                    for h in range(4):
                        ps = slice(64 * (h % 2), 64 * (h % 2) + 64)
                        ins = e.matmul(bS[h % 2][:, (h // 2) * 128:(h // 2 + 1) * 128],
                                       lhsT=self.KmT[ps, h // 2, mc],
                                       rhs=self.QT[ps, qs, t, 6 + h // 2, :], start=True, stop=True)
                    return ins

                def pv(e, pt, mt=mt, o3=o3):
                    for h in range(4):
                        hp = (h % 2) * 2 + h // 2
                        ins = e.matmul(o3[:, h, :], lhsT=pt[:, hp * 128:(hp + 1) * 128],
                                       rhs=self.Vm[:, mt, h, :], start=(mt == 0 and h == 0), stop=(mt == 1),
                                       skip_group_check=True)
                    return ins
                fin = (bO, bOk, o3, 768, None, None, 3, ln) if mt == 1 else None
                units.append(dict(smm=smm, skeys=["KmT", QTk], negB=negBm, emask=None, ekey=None, split=True,
                                  pv=pv, pvkeys=["Vm"], bOk=bOk, fin=fin, epi=(t if mt == 1 else None)))

        def back(u):
            pt, ptk = u["pt"]
            self.op("pe", lambda e, u=u, pt=pt: u["pv"](e, pt), [ptk] + u["pvkeys"], [u["bOk"]], lag=1)
            if u["fin"] is not None:
                self.finish_heads(*u["fin"])
            for pe_ in pend_epi:
                pe_[0] -= 1
            while pend_epi and pend_epi[0][0] <= 0:
                pend_epi.pop(0)[1]()
            if u["epi"] is not None:
                pend_epi.append([3, self.attn_epilogue(u["epi"])])
        pend_epi = []
        prev = None
        for u in units:
            u["pt"] = self.attn_unit(u["smm"], u["skeys"], u["negB"], u["emask"], u["ekey"],
                                     "split" if u["split"] else None, None, None)
            if prev is not None:
                back(prev)
            prev = u
        back(prev)
        for pe_ in pend_epi:
            pe_[1]()
        for n in range(2):
            s = self.load_slab(self.wout_s[l, n].rearrange("p k n -> p (k n)"), 4096)
            wsl = self.wslab[:, s, :].rearrange("p (k n) -> p k n", k=8)
            for t in range(4):
                kb = self.rOP.next()
                bM, bMk = self.bankF[kb], ("bk", kb)
                tc = slice(t * 128, (t + 1) * 128)

                def mm(e, bM=bM, wsl=wsl, tc=tc):
                    for k in range(8):
                        ins = e.matmul(bM[:], lhsT=self.hT[:, k, tc], rhs=wsl[:, k, :], start=(k == 0),
                                       stop=(k == 7))
                    return ins
                self.op("pe", mm, [("hT", t), ("ws", s)], [bMk])
                xv = self.xbuf[:, xs_, t, n * 512:(n + 1) * 512]
                self.op("dve", lambda e, xv=xv, bM=bM: e.tensor_tensor(out=xv, in0=bM[:], in1=xv, op=ALU.add),
                        [bMk, ("xb", xs_, t)], [("xb", xs_, t)])

    def A_xnorm(self, i):
        xs_ = i % 2

        self.norm_pipe([(self.xbuf[:, xs_, t, :], [("xb", xs_, t)], self.oxT[:, :, t * 128:(t + 1) * 128],
                         [("oxT", t)], None) for t in range(4)])

    def attn_epilogue(self, t):
        ln = t % 2
        tc = slice(t * 128, (t + 1) * 128)
        of32, obf, sso = self.of32[:, ln, :], self.obf[:, ln, :], self.sso[:, ln, :]
        for gi, (c0, w) in enumerate(((0, 512), (512, 256), (768, 256))):
            rk = [("of32", ln, 0), ("of32", ln, 1)] if gi == 0 else [("of32", ln, gi + 1)]
            self.op("act", lambda e, c0=c0, w=w, gi=gi: e.activation(
                out=obf[:, c0:c0 + w], in_=of32[:, c0:c0 + w], func=AF.Square,
                accum_out=sso[:, gi:gi + 1]), rk, [("sso", ln, gi), ("obf", ln, gi)])
        for gi, (c0, w) in enumerate(((0, 512), (512, 256), (768, 256))):
            self.rstd_chain(sso[:, gi:gi + 1], sso[:, gi:gi + 1], w, [("sso", ln, gi)], [("sso", ln, gi)])
            self.op("dve", lambda e, c0=c0, w=w, gi=gi: e.tensor_scalar(
                out=obf[:, c0:c0 + w], in0=of32[:, c0:c0 + w], scalar1=sso[:, gi:gi + 1],
                scalar2=None, op0=ALU.mult), [("sso", ln, gi)] + [("of32", ln, q) for q in range(4)],
                [("obf", ln, gi)])
        def back():
            kb = self.rS4.next()
            bT, bk = self.bankF[kb][:].bitcast(BF16), ("bk", kb)

            def tr(e, bT=bT):
                for k in range(8):
                    ins = e.transpose(out=bT[:, k * 128:(k + 1) * 128], in_=obf[:, k * 128:(k + 1) * 128],
                                      identity=self.ident[:])
                return ins
            self.op("pe", tr, [("obf", ln, 0), ("obf", ln, 1), ("obf", ln, 2), "ident"], [bk], lag=2)
            self.op("act", lambda e, bT=bT, tc=tc: e.activation(
                out=self.hT[:, :, tc], in_=bT.rearrange("p (k n) -> p k n", k=8), func=AF.Copy),
                [bk], [("hT", t)])
        return back

    def A_ff1(self, l, i):
        oxk = [("oxT", t) for t in range(4)]
        for sl in range(8):
            s = self.load_slab(self.w1_s[l, :, sl].rearrange("p j k h -> p (j k h)"), 4096, "X")
            wsl = self.wslab[:, s, :].rearrange("p (j k h) -> p j k h", j=4, k=8)
            for jj in range(4):
                j = sl * 4 + jj
                kb = 4 + self.rF1.next()
                bM, bMk = self.bankF[kb], ("bk", kb)

                for half in range(2):
                    def mm(e, bM=bM, wsl=wsl, jj=jj, half=half):
                        for k in range(4 * half, 4 * half + 4):
                            ins = e.matmul(bM[:], lhsT=wsl[:, jj, k, :], rhs=self.oxT[:, k, :], start=(k == 0),
                                           stop=(k == 7))
                        return ins
                    self.op("pe", mm, oxk + [("ws", s)], [bMk])
                r = self.rrt.next()
                rt = self.rtmp[:, r, :]
                self.op("act", lambda e, rt=rt, bM=bM: e.activation(out=rt, in_=bM[:], func=AF.Relu),
                        [bMk], [("rtmp", r)])
                self.op("pool", lambda e, rt=rt, j=j: e.tensor_tensor(out=self.gT[:, j, :], in0=rt, in1=rt,
                                                                      op=ALU.mult),
                        [("rtmp", r)], [("gT", j)])

    def A_ff2(self, l, i, dst, dstk):
        xs_ = i % 2
        G = [(self.bankF[2 + t], ("bk", 2 + t)) for t in range(4)]
        for n in range(2):
            for ns in range(4):
                s = self.load_slab(self.w2_s[l, n, ns].rearrange("p j c -> p (j c)"), 4096, "X")
                wsl = self.wslab[:, s, :].rearrange("p (j c) -> p j c", j=8)

                for jj in range(8):
                    j = ns * 8 + jj

                    def mm(e, wsl=wsl, j=j, jj=jj):
                        for t in range(4):
                            ins = e.matmul(G[t][0][:], lhsT=self.gT[:, j, t * 128:(t + 1) * 128],
                                           rhs=wsl[:, jj, :], start=(j == 0), stop=(j == 31))
                        return ins
                    self.op("pe", mm, [("gT", j), ("ws", s)], [G[t][1] for t in range(4)])
            for t in range(4):
                xv = self.xbuf[:, xs_, t, n * 512:(n + 1) * 512]
                self.op("dve", lambda e, xv=xv, b=G[t][0]: e.tensor_tensor(out=xv, in0=b[:], in1=xv, op=ALU.add),
                        [G[t][1], ("xb", xs_, t)], [("xb", xs_, t)])
        self.dma("pool", dst[i * 512:(i + 1) * 512, :].rearrange("(t p) d -> p t d", p=128), self.xbuf[:, xs_],
                 [("xb", xs_, t) for t in range(4)], [dstk + (i,)], ("st", xs_))

    def finish_heads(self, bO, bOk, o3, col0, sink_ap, sink_key, gi, ln):
        den = self.dtmp[:, ln, gi, :]
        dk = ("den", ln, gi)
        if sink_ap is not None:
            self.op("dve", lambda e: e.tensor_tensor(out=den, in0=o3[:, :, 64], in1=sink_ap, op=ALU.add),
                    [bOk, sink_key], [dk])
        else:
            self.op("dve", lambda e: e.tensor_scalar(out=den, in0=o3[:, :, 64], scalar1=0.0, scalar2=None,
                                                     op0=ALU.add), [bOk], [dk])
        self.op("dve", lambda e: e.reciprocal(out=den, in_=den), [dk], [dk])
        self.op("dve", lambda e: e.tensor_tensor(
            out=self.of32[:, ln, col0:col0 + 256].rearrange("p (h d) -> p h d", h=4), in0=o3[:, :, 0:64],
            in1=den.unsqueeze(2).to_broadcast([128, 4, 64]), op=ALU.mult),
            [bOk, dk], [("of32", ln, gi)])


def _const_ea():
    kj = np.arange(128)[:, None]
    qi = np.arange(128)[None, :]
    out = np.zeros((128, 6, 4, 128), np.float32)
    for g in range(2):
        for hh in range(4):
            slope = 2.0 ** (-(4 * g + hh + 1))
            for ri, rel in enumerate((-1, 0, 1)):
                if rel == -1:
                    dist = 128 + qi - kj
                elif rel == 0:
                    dist = np.abs(qi - kj)
                else:
                    dist = 128 + kj - qi
                valid = dist <= 128
                out[:, g * 3 + ri, hh, :] = np.where(valid, np.exp(-slope * dist), 0.0)
    return out.reshape(128, 6 * 512).astype(ml_dtypes.bfloat16)


def _const_nmask():
    out = np.zeros((128, NMK, 128), np.float32)
    c = np.arange(64)
    cstart = np.clip(c - 8, 0, 48)
    colvalid = (c[None, :] >= cstart[:, None]) & (c[None, :] < cstart[:, None] + 16)
    for mi, j in enumerate((None, -2, 2)):
        for a in range(2):
            for b in range(2):
                ok = True if j is None else (-4 <= 2 * j + b - a <= 3)
                if ok:
                    out[b * 64:(b + 1) * 64, mi, a * 64:(a + 1) * 64] = colvalid.T.astype(np.float32)
    return out.reshape(128, NMK * 128).astype(ml_dtypes.bfloat16)


def _layout_biasT(rpb):
    L = rpb.shape[0]
    b = np.arange(2)[:, None, None, None, None]
    kc = np.arange(64)[None, :, None, None, None]
    j = np.arange(-3, 4)[None, None, :, None, None]
    a = np.arange(2)[None, None, None, :, None]
    qc = np.arange(64)[None, None, None, None, :]
    dr = np.clip(2 * j + b - a + 7, 0, 14) + 0 * kc + 0 * qc
    dc = np.clip(kc - qc + 15, 0, 30) + 0 * b + 0 * j + 0 * a
    g = rpb[:, :, dr, dc]
    g = np.transpose(g, (0, 2, 3, 1, 4, 5, 6))
    return np.ascontiguousarray(g.reshape(L, 128, 4 * 7 * 128)).astype(np.float32)


def _shared_inputs(g_mix, w_in, qk_gain, sink, rpb, o_gain, w_out, g_mem, w_mem_kv, g_ff, w_ff1, w_ff2):
    L = w_in.shape[0]
    gvs = np.stack([g_mix, o_gain, g_mem, g_ff], axis=1)
    gcols = gvs.reshape(L, 4, 8, 128).transpose(3, 0, 1, 2).reshape(128, L * 4 * 8)
    gaincols = np.tile(qk_gain.transpose(2, 0, 1).reshape(64, L * 6), (2, 1))
    return {
        "w_in": np.ascontiguousarray(w_in, np.float32), "w_out": np.ascontiguousarray(w_out, np.float32),
        "w_mem_kv": np.ascontiguousarray(w_mem_kv, np.float32),
        "w_ff1": np.ascontiguousarray(w_ff1, np.float32), "w_ff2": np.ascontiguousarray(w_ff2, np.float32),
        "gcols": np.ascontiguousarray(gcols, np.float32),
        "gaincols": np.ascontiguousarray(gaincols, np.float32),
        "gainrows": np.ascontiguousarray(qk_gain.reshape(1, L * 6 * 64), np.float32),
        "sinkrow": np.ascontiguousarray(sink.reshape(1, L * 8), np.float32),
        "biasT": _layout_biasT(np.asarray(rpb, np.float32)),
        "ident": np.eye(128, dtype=np.float32).astype(ml_dtypes.bfloat16),
        "ea": _const_ea(), "nmask": _const_nmask(),
    }


_NC_CACHE = {}


def _get_nc(segs, depth):
    key = (tuple(segs), depth)
    if key not in _NC_CACHE:
        _NC_CACHE[key] = Builder(list(segs), depth).build()
    return _NC_CACHE[key]


def kernel(x_prompt, x_sample, mem_prompt, mem_sample, g_mix, w_in, qk_gain, sink, rpb,
           o_gain, w_out, g_mem, w_mem_kv, g_ff, w_ff1, w_ff2):
    x_prompt = np.asarray(x_prompt, np.float32)
    x_sample = np.asarray(x_sample, np.float32)
    mem_prompt = np.asarray(mem_prompt, np.float32)
    mem_sample = np.asarray(mem_sample, np.float32)
    BP, SP = x_prompt.shape[0], x_prompt.shape[1]
    BS, SS = x_sample.shape[0], x_sample.shape[1]
    depth = w_in.shape[0]
    assert BP == NCORES and BS * 2 == NCORES and depth == 2
    HALF = SS // 2
    HALO = 512
    SEG = HALF + HALO
    segs = (("p", SP), ("s", SEG))
    nc = _get_nc(segs, depth)
    shared = _shared_inputs(*[np.asarray(a, np.float32) for a in
                              (g_mix, w_in, qk_gain, sink, rpb, o_gain, w_out, g_mem, w_mem_kv, g_ff,
                               w_ff1, w_ff2)])
    in_maps = []
    for c in range(NCORES):
        b, h = c % BS, c // BS
        lo = 0 if h == 0 else SS - SEG
        m = dict(shared)
        m["x_p"] = np.ascontiguousarray(x_prompt[c])
        m["mem_p"] = np.ascontiguousarray(mem_prompt[c])
        m["x_s"] = np.ascontiguousarray(x_sample[b, lo:lo + SEG])
        m["mem_s"] = np.ascontiguousarray(mem_sample[b])
        in_maps.append(m)
    res = run_bass_kernel_spmd(nc, in_maps, core_ids=list(range(NCORES)))
    y_p = np.stack([np.asarray(res.results[c]["y_p"], np.float32) for c in range(BP)], axis=0)
    y_s = np.empty((BS, SS, D), np.float32)
    for c in range(NCORES):
        b, h = c % BS, c // BS
        ys = np.asarray(res.results[c]["y_s"], np.float32)
        if h == 0:
            y_s[b, 0:HALF] = ys[0:HALF]
        else:
            y_s[b, HALF:SS] = ys[SEG - HALF:SEG]
    return (y_p, y_s)
```

```python
import numpy as np
from contextlib import ExitStack
import ml_dtypes
import concourse.bass as bass
import concourse.mybir as mybir
from concourse.bass_utils import run_bass_kernel_spmd

F32 = mybir.dt.float32
BF16 = mybir.dt.bfloat16
AF = mybir.ActivationFunctionType
ALU = mybir.AluOpType
AX = mybir.AxisListType

D = 1024
INW = 1792
DFF = 4096
NMEM = 256
EPS = 1e-6
NCORES = 8
EPOCH = 24000

NV = 9
NMK = 3
NB_VSPEC = [(j + 3, 0) for j in range(-3, 4)] + [(1, 1), (5, 2)]


def nb_variant(case, j):
    if case == "int" and j == -2:
        return 7
    if case == "int" and j == 2:
        return 8
    return j + 3


class _Op:
    __slots__ = ("eng", "fn", "reads", "writes", "lane", "lane_val", "deps", "signal", "sig_idx", "lag")

    def __init__(self, eng, fn, reads, writes, lane):
        self.eng, self.fn, self.reads, self.writes, self.lane = eng, fn, reads, writes, lane
        self.lane_val = None
        self.deps = ()
        self.signal = False
        self.sig_idx = None
        self.lag = 0


class Prog:
    ENGS = ("pe", "act", "dve", "pool", "sp")

    def __init__(self):
        self.ops = []

    def add(self, eng, fn, reads=(), writes=(), lane=None):
        self.ops.append(_Op(eng, fn, tuple(reads), tuple(writes), lane))

    def analyze(self):
        last_writer, readers, lane_last, lane_cnt = {}, {}, {}, {}
        for i, op in enumerate(self.ops):
            deps = set()
            for r in op.reads:
                j = last_writer.get(r)
                if j is not None:
                    deps.add(j)
                if isinstance(r, tuple) and r[0] in ("bk", "bkT"):
                    for j in readers.get(r, ()):
                        if self.ops[j].eng != op.eng:
                            deps.add(j)
            for w in op.writes:
                j = last_writer.get(w)
                if j is not None:
                    deps.add(j)
                deps.update(readers.get(w, ()))
            if op.lane is not None:
                j = lane_last.get(op.lane)
                if j is not None:
                    deps.add(j)
                lane_last[op.lane] = i
                lane_cnt[op.lane] = lane_cnt.get(op.lane, 0) + 1
                op.lane_val = 16 * lane_cnt[op.lane]
            deps.discard(i)
            real = []
            for j in deps:
                oj = self.ops[j]
                if oj.lane is None and oj.eng == "pe" and op.eng == "pe" and op.lane is None:
                    continue
                real.append(j)
                if oj.lane is None:
                    oj.signal = True
            op.deps = real
            for r in op.reads:
                readers.setdefault(r, []).append(i)
            for w in op.writes:
                last_writer[w] = i
                readers[w] = []
        cnt = {e: 0 for e in self.ENGS}
        for op in self.ops:
            if op.signal:
                op.sig_idx = cnt[op.eng]
                cnt[op.eng] += 1
        self.sig_counts = cnt
        self.lanes = list(lane_cnt.keys())

    def emit(self, nc, es):
        self.analyze()
        sems = {}
        for e in self.ENGS:
            for ep in range(self.sig_counts[e] // EPOCH + 1):
                sems[("eng", e, ep)] = es.enter_context(nc.semaphore(f"s_{e}_{ep}"))
        for k, lane in enumerate(self.lanes):
            sems[("lane", lane)] = es.enter_context(nc.semaphore(f"l_{k}"))
        block = es.enter_context(nc.Block())
        per_eng = {e: [op for op in self.ops if op.eng == e] for e in self.ENGS}
        ops = self.ops

        def run(engname, e):
            waited = {}
            for op in per_eng[engname]:
                need = {}
                for j in op.deps:
                    oj = ops[j]
                    if oj.lane is not None:
                        key = ("lane", oj.lane)
                        val = oj.lane_val
                    else:
                        key = ("eng", oj.eng)
                        val = oj.sig_idx + 1
                    if need.get(key, 0) < val:
                        need[key] = val
                for key, val in need.items():
                    if waited.get(key, 0) >= val:
                        continue
                    waited[key] = val
                    if key[0] == "lane":
                        e.wait_ge(sems[key], val)
                    else:
                        ep, loc = divmod(val - 1, EPOCH)
                        e.wait_ge(sems[("eng", key[1], ep)], loc + 1)
                ins = op.fn(e)
                if op.lane is not None:
                    ins.then_inc(sems[("lane", op.lane)], 16)
                elif op.signal:
                    ep = op.sig_idx // EPOCH
                    ins.then_inc(sems[("eng", engname, ep)], 1)

        @block.sync
        def _(e):
            run("sp", e)

        @block.gpsimd
        def _(e):
            run("pool", e)

        @block.scalar
        def _(e):
            run("act", e)

        @block.vector
        def _(e):
            run("dve", e)

        @block.tensor
        def _(e):
            run("pe", e)


class _Rot:
    def __init__(self, n):
        self.n, self.i = n, -1

    def next(self):
        self.i = (self.i + 1) % self.n
        return self.i


class Builder:
    def __init__(self, segs, depth=2):
        self.segs = segs
        self.depth = depth
        self.nc = bass.Bass("TRN2", target_bir_lowering=False)
        self.P = Prog()
        self.lane = 0

    def op(self, eng, fn, reads=(), writes=(), lane=None, lag=0):
        self.P.add(eng, fn, reads, writes, lane)
        self.P.ops[-1].lag = lag

    def streams(self, fns):
        lists = []
        lane0 = self.lane
        for k, fn in enumerate(fns):
            saved = self.P.ops
            self.P.ops = []
            self.lane = k
            fn()
            lists.append(self.P.ops)
            self.P.ops = saved
        self.lane = lane0
        idx = [0] * len(lists)
        alive = True
        while alive:
            alive = False
            for k, lst in enumerate(lists):
                if idx[k] < len(lst):
                    self.P.ops.append(lst[idx[k]])
                    idx[k] += 1
                    alive = True

    def merge_prop(self, f_main, f_fill):
        lists = []
        for fn in (f_main, f_fill):
            saved = self.P.ops
            self.P.ops = []
            fn()
            lists.append(self.P.ops)
            self.P.ops = saved
        M, Fl = lists
        out = self.P.ops
        ix = 0
        for iy, y in enumerate(M):
            if y.lag:
                need = y.lag
                while need > 0 and ix < len(Fl):
                    x = Fl[ix]
                    out.append(x)
                    ix += 1
                    if x.eng == "pe":
                        need -= 1
            out.append(y)
        out.extend(Fl[ix:])

    def dma(self, eng, out, in_, reads, writes, lane, **kw):
        self.op(eng, lambda e, out=out, in_=in_, kw=kw: e.dma_start(out=out, in_=in_, **kw),
                reads, writes, lane)

    def build(self):
        nc = self.nc
        L = self.depth
        with ExitStack() as es:
            self.es = es
            self._declare_dram()
            self._alloc()
            self._startup()
            for si, (name, S) in enumerate(self.segs):
                for l in range(L):
                    src = self.x_in[si] if l == 0 else self.x_mid[si][(l - 1) % 2]
                    dst = self.y_out[si] if l == L - 1 else self.x_mid[si][l % 2]
                    srck = ("din", si) if l == 0 else ("dmid", si, (l - 1) % 2)
                    dstk = ("dout", si) if l == L - 1 else ("dmid", si, l % 2)
                    self._layer_pass(si, S, l, src, dst, srck, dstk)
            fin_reads = []
            for si, (name, S) in enumerate(self.segs):
                fin_reads += [("dout", si, c) for c in range(S // 512)]
            self.op("pool", lambda e: e.nop(), reads=fin_reads)
            self.P.emit(nc, es)
        return nc

    def _declare_dram(self):
        nc, L = self.nc, self.depth
        dt = nc.dram_tensor
        self.x_in, self.y_out, self.x_mid, self.mem_in = [], [], [], []
        for si, (name, S) in enumerate(self.segs):
            self.x_in.append(dt(f"x_{name}", [S, D], F32, kind="ExternalInput").ap())
            self.mem_in.append(dt(f"mem_{name}", [NMEM, D], F32, kind="ExternalInput").ap())
            self.y_out.append(dt(f"y_{name}", [S, D], F32, kind="ExternalOutput").ap())
            self.x_mid.append([dt(f"xmid_{name}_{k}", [S, D], F32, kind="Internal").ap()
                               for k in range(min(2, max(L - 1, 1)))])
        self.w_in = dt("w_in", [L, D, INW], F32, kind="ExternalInput").ap()
        self.w_out = dt("w_out", [L, D, D], F32, kind="ExternalInput").ap()
        self.w_mem = dt("w_mem_kv", [L, D, 512], F32, kind="ExternalInput").ap()
        self.w_ff1 = dt("w_ff1", [L, D, DFF], F32, kind="ExternalInput").ap()
        self.w_ff2 = dt("w_ff2", [L, DFF, D], F32, kind="ExternalInput").ap()
        self.gcols_d = dt("gcols", [128, L * 4 * 8], F32, kind="ExternalInput").ap()
        self.gaincols_d = dt("gaincols", [128, L * 6], F32, kind="ExternalInput").ap()
        self.gainrows_d = dt("gainrows", [1, L * 6 * 64], F32, kind="ExternalInput").ap()
        self.sink_d = dt("sinkrow", [1, L * 8], F32, kind="ExternalInput").ap()
        self.biasT_d = dt("biasT", [L, 128, 4 * 7 * 128], F32, kind="ExternalInput").ap()
        self.ident_d = dt("ident", [128, 128], BF16, kind="ExternalInput").ap()
        self.ea_d = dt("ea", [128, 6 * 512], BF16, kind="ExternalInput").ap()
        self.nmask_d = dt("nmask", [128, NMK * 128], BF16, kind="ExternalInput").ap()
        self.win_s = dt("win_s", [L, 4, 128, 8, 512], BF16, kind="Internal").ap()
        self.wout_s = dt("wout_s", [L, 2, 128, 8, 512], BF16, kind="Internal").ap()
        self.wm_s = dt("wm_s", [L, 128, 8, 512], BF16, kind="Internal").ap()
        self.w1_s = dt("w1_s", [L, 128, 8, 4, 8, 128], BF16, kind="Internal").ap()
        self.w2_s = dt("w2_s", [L, 2, 4, 128, 8, 512], BF16, kind="Internal").ap()

    def _alloc(self):
        nc, es, L = self.nc, self.es, self.depth

        def sb(name, shape, dtype):
            return es.enter_context(nc.sbuf_tensor(name, shape, dtype))

        self.ident = sb("ident_sb", [128, 128], BF16)
        self.ea = sb("ea_sb", [128, 6, 512], BF16)
        self.enb = sb("enb", [128, NV, 4, 128], BF16)
        self.gcols = sb("gcols_sb", [128, L * 4 * 8], F32)
        self.gaincols = sb("gaincols_sb", [128, L * 6], F32)
        self.gainrows = sb("gainrows_sb", [128, L * 6, 64], F32)
        self.gprod = sb("gprod", [128, L * 3, 64], F32)
        self.gs = sb("gs", [128, L * 3], F32)
        self.negB = sb("negB", [128, L * 3], F32)
        self.sinkexp = sb("sinkexp", [128, L * 8], F32)
        self.epsc = sb("epsc", [128, 1], F32)
        self.xbuf = sb("xbuf", [128, 2, 4, D], F32)
        self.xst = sb("xst", [128, 2, D], F32)
        self.xsb = sb("xsb", [128, 3, D], BF16)
        self.hT = sb("hT", [128, 8, 512], BF16)
        self.oxT = sb("oxT", [128, 8, 512], BF16)
        self.QT = sb("QT", [128, 2, 4, 8, 128], BF16)
        self.KT = sb("KT", [128, 3, 3, 512], BF16)
        self.Vb = sb("Vb", [128, 3, 4, 6, 65], BF16)
        self.KmT = sb("KmT", [128, 2, 256], BF16)
        self.Vm = sb("Vm", [128, 2, 4, 65], BF16)
        self.praw = sb("praw", [128, 2, 512], BF16)
        self.qkn = sb("qkn", [128, 2, 512], BF16)
        self.sqb = sb("sqb", [128, 2, 512], BF16)
        self.ssq = sb("ssq", [128, 2, 8], F32)
        self.rsq = sb("rsq", [128, 2, 8], F32)
        self.pexp = sb("pexp", [128, 4, 512], BF16)
        self.of32 = sb("of32", [128, 2, D], F32)
        self.estage = self.of32[:, 0, 0:896].rearrange("p (j n) -> p j n", j=7)
        self.obf = sb("obf", [128, 2, D], BF16)
        self.dtmp = sb("dtmp", [128, 2, 4, 4], F32)
        self.sso = sb("sso", [128, 2, 4], F32)
        self.ssx = sb("ssx", [128, 3, 1], F32)
        self.gT = sb("gT", [128, 32, 512], BF16)
        self.nmask = sb("nmask_sb", [128, NMK, 128], BF16)
        self.rtmp = sb("rtmp", [128, 2, 512], BF16)
        self.wslab = sb("wslab", [128, 4, 4096], BF16)
        self.bankF = [es.enter_context(nc.psum_tensor(f"bF{k}", [128, 512], F32)) for k in range(6)]
        self.bankT = [es.enter_context(nc.psum_tensor(f"bT{k}", [128, 1024], BF16)) for k in range(2)]
        self.rSl = [_Rot(2), _Rot(2)]
        self.rO = _Rot(3)
        self.rOP = _Rot(2)
        self.rF1 = _Rot(2)
        self.rX = _Rot(2)
        self.rpe4 = _Rot(4)
        self.rpu = _Rot(2)
        self.rPM = _Rot(2)
        self.pu_pending = None
        self.rS4 = _Rot(4)
        self.rrt = _Rot(2)
        self.rwsY, self.rwsX = _Rot(2), _Rot(2)
        self.flip = 0

    def bank_S(self):
        k = self.rS4.next()
        return self.bankF[k], ("bk", k)

    def bank_M(self):
        k = self.rPM.next()
        return self.bankF[k], ("bk", k)

    def bank_X(self):
        k = 2 + self.rX.next()
        return self.bankF[k], ("bk", k)

    def bank_M3(self):
        k = self.rO.next()
        return self.bankF[k], ("bk", k)

    def bank_O(self):
        k = 4 + self.lane
        return self.bankF[k], ("bk", k)

    def bank_T(self):
        k = self.lane
        if k == 2:
            return self.bankF[4][:].bitcast(BF16), ("bk", 4)
        return self.bankT[k], ("bkT", k)

    def _startup(self):
        L = self.depth
        op, dma = self.op, self.dma
        dma("pool", self.ident[:], self.ident_d, [], ["ident"], "c0")
        dma("pool", self.ea[:], self.ea_d.rearrange("p (a n) -> p a n", a=6), [], ["ea"], "c1")
        dma("pool", self.nmask[:], self.nmask_d.rearrange("p (v n) -> p v n", v=NMK), [], ["nmask"], "c2")
        dma("pool", self.gcols[:], self.gcols_d, [], ["gcols"], "c3")
        dma("pool", self.gaincols[:], self.gaincols_d, [], ["gaincols"], "c4")
        dma("pool", self.gainrows[:].rearrange("p a d -> p (a d)"),
            self.gainrows_d[0].partition_broadcast(128), [], ["gainrows"], "c5")
        dma("pool", self.sinkexp[:], self.sink_d[0].partition_broadcast(128), [], ["sinkraw"], "c6")
        op("dve", lambda e: e.memset(self.epsc[:], EPS), [], ["epsc"])
        op("dve", lambda e: e.memset(self.Vb[:, :, :, :, 64:65], 1.0), [],
           [("V", s, t) for s in range(3) for t in range(4)])
        op("dve", lambda e: e.memset(self.Vm[:, :, :, 64:65], 1.0), [], ["Vm"])
        for l in range(L):
            for ty in range(3):
                c = l * 3 + ty
                a, b = l * 6 + 2 * ty, l * 6 + 2 * ty + 1
                op("dve", lambda e, c=c, a=a, b=b: e.scalar_tensor_tensor(
                    out=self.gs[:, c:c + 1], in0=self.gaincols[:, a:a + 1], scalar=0.125,
                    in1=self.gaincols[:, b:b + 1], op0=ALU.mult, op1=ALU.mult),
                   ["gaincols"], ["gs"])
                op("dve", lambda e, c=c, a=a, b=b: e.tensor_tensor(
                    out=self.gprod[:, c, :], in0=self.gainrows[:, a, :], in1=self.gainrows[:, b, :],
                    op=ALU.mult), ["gainrows"], [("gprod", c)])
                op("dve", lambda e, c=c: e.tensor_reduce(
                    out=self.negB[:, c:c + 1], in_=self.gprod[:, c, :], axis=AX.X, op=ALU.max,
                    apply_absolute_value=True), [("gprod", c)], ["negB"])
                op("dve", lambda e, c=c: e.tensor_scalar(
                    out=self.negB[:, c:c + 1], in0=self.negB[:, c:c + 1], scalar1=-8.0, scalar2=None,
                    op0=ALU.mult), ["negB"], ["negB"])
            op("act", lambda e, l=l: e.activation(
                out=self.sinkexp[:, l * 8:(l + 1) * 8], in_=self.sinkexp[:, l * 8:(l + 1) * 8], func=AF.Exp,
                bias=self.negB[:, l * 3:l * 3 + 1], scale=1.0),
               ["sinkraw", "negB"], ["sinkexp"])
        self._prep_weights()

    def _prep_weights(self):
        L = self.depth
        op, dma = self.op, self.dma
        xb = self.xbuf
        gt = self.gT
        rr = _Rot(4)
        eng_flip = [0]
        self.wscr_keys = []

        def stage(src_ap, ncols, gcol, stores, a3=None):
            r = rr.next()
            if r < 2:
                a = xb[:, r].rearrange("p t d -> p (t d)")[:, 0:ncols]
                xk = [("xb", r, t) for t in range(4)]
            else:
                q = 2 * (r - 2)
                a = self.wslab[:, q:q + 2, :].rearrange("p s n -> p (s n)").bitcast(F32)[:, 0:ncols]
                xk = [("ws", q), ("ws", q + 1)]
            a_dst = a if a3 is None else a3(a)
            b = gt[:, r * 8:(r + 1) * 8, :].rearrange("p j n -> p (j n)")[:, 0:ncols]
            gk = [("gT", j) for j in range(r * 8, (r + 1) * 8)]
            dma("sp", a_dst, src_ap, [], xk, ("pl", r))
            eng = "dve" if eng_flip[0] % 2 == 0 else "act"
            eng_flip[0] += 1
            if eng == "dve":
                if gcol is None:
                    op("dve", lambda e, a=a, b=b: e.tensor_copy(out=b, in_=a), xk, gk)
                else:
                    op("dve", lambda e, a=a, b=b, g=gcol: e.tensor_scalar(
                        out=b, in0=a, scalar1=g, scalar2=None, op0=ALU.mult), xk + ["gcols"], gk)
            else:
                if gcol is None:
                    op("act", lambda e, a=a, b=b: e.activation(out=b, in_=a, func=AF.Copy), xk, gk)
                else:
                    op("act", lambda e, a=a, b=b, g=gcol: e.activation(
                        out=b, in_=a, func=AF.Copy, scale=g), xk + ["gcols"], gk)
            for n, (dst, srcview) in enumerate(stores):
                wk_ = ("wscr", len(self.wscr_keys))
                self.wscr_keys.append(wk_)
                dma("sp", dst, srcview(b), gk, [wk_], ("ps", r, n))

        for l in range(L):
            def gc(v, k, l=l):
                c = (l * 4 + v) * 8 + k
                return self.gcols[:, c:c + 1]
            for k in range(8):
                rows = slice(k * 128, (k + 1) * 128)
                stage(self.w_in[l, rows, :], INW, gc(0, k), [
                    (self.win_s[l, 0:3, :, k, :].rearrange("s p n -> p s n"),
                     lambda b: b[:, 0:1536].rearrange("p (s n) -> p s n", s=3)),
                    (self.win_s[l, 3, :, k, 0:256], lambda b: b[:, 1536:1792]),
                ])
                stage(self.w_out[l, rows, :], D, gc(1, k), [
                    (self.wout_s[l, :, :, k, :].rearrange("s p n -> p s n"),
                     lambda b: b[:, 0:1024].rearrange("p (s n) -> p s n", s=2)),
                ])
                stage(self.w_mem[l, rows, :], 512, gc(2, k), [
                    (self.wm_s[l, :, k, :], lambda b: b[:, 0:512]),
                ])
                stage(self.w_ff1[l, rows, :], DFF, gc(3, k), [
                    (self.w1_s[l, :, :, :, k, :].rearrange("p s j h -> p (s j) h"),
                     lambda b: b[:, 0:4096].rearrange("p (sj h) -> p sj h", sj=32)),
                ])
            for jg in range(8):
                ns, jj0 = jg // 2, (jg % 2) * 4
                src = self.w_ff2[l, jg * 512:(jg + 1) * 512, :].rearrange("(j p) c -> p j c", p=128)
                stage(src, 4096, None, [
                    (self.w2_s[l, n, ns, :, jj0:jj0 + 4, :],
                     (lambda b, n=n: b[:, 0:4096].rearrange("p (j c) -> p j c", j=4)[:, :, n * 512:(n + 1) * 512]))
                    for n in range(2)
                ], a3=lambda a: a.rearrange("p (j c) -> p j c", j=4))

    def slab_slot(self, ring):
        return self.rwsY.next() if ring == "Y" else 2 + self.rwsX.next()

    def load_slab(self, src_ap, ncols, ring="Y"):
        s = self.slab_slot(ring)
        dst = self.wslab[:, s, 0:ncols]
        self.dma("sp", dst, src_ap, self.wscr_keys, [("ws", s)], ("ws", s))
        return s

    def rstd_chain(self, ss_ap, out_ap, n, rk, wk):
        self.op("act", lambda e: e.activation(out=out_ap, in_=ss_ap, func=AF.Ln, scale=1.0 / n,
                                              bias=self.epsc[:, 0:1]), list(rk) + ["epsc"], list(wk))
        self.op("act", lambda e: e.activation(out=out_ap, in_=out_ap, func=AF.Exp, scale=-0.5),
                list(wk), list(wk))

    def norm_pipe(self, items):
        pend = None
        for k, (src_ap, src_keys, dstT_ap, dst_keys, pre) in enumerate(items):
            if pre is not None:
                pre()
            back = self.norm_transpose(src_ap, src_keys, dstT_ap, dst_keys, slot=k % 3, tbank=k % 2, defer=True)
            if pend is not None:
                pend()
            pend = back
        if pend is not None:
            pend()

    def norm_transpose(self, src_ap, src_keys, dstT_ap, dst_keys, slot=None, tbank=None, defer=False):
        r = self.lane if slot is None else slot
        ss = self.ssx[:, r, :]
        xr = r
        xs = self.xsb[:, xr, :]
        self.op("act", lambda e: e.activation(out=xs, in_=src_ap, func=AF.Square, accum_out=ss),
                src_keys, [("ssx", r), ("xsb", xr)])
        self.rstd_chain(ss, ss, D, [("ssx", r)], [("ssx", r)])
        self.op("dve", lambda e: e.tensor_scalar(out=xs, in0=src_ap, scalar1=ss, scalar2=None, op0=ALU.mult),
                list(src_keys) + [("ssx", r)], [("xsb", xr)])
        if tbank is None:
            bT, bk = self.bank_T()
        else:
            bT, bk = self.bankT[tbank], ("bkT", tbank)

        def tr(e):
            for k in range(8):
                ins = e.transpose(out=bT[:, k * 128:(k + 1) * 128], in_=xs[:, k * 128:(k + 1) * 128],
                                  identity=self.ident[:])
            return ins

        def back():
            self.op("pe", tr, [("xsb", xr), "ident"], [bk], lag=3)
            self.op("act", lambda e: e.activation(out=dstT_ap, in_=bT[:].rearrange("p (k n) -> p k n", k=8),
                                                  func=AF.Copy), [bk], dst_keys)
        if defer:
            return back
        back()

    def proj_unit(self, bank, bkey, norms, vcopies, pair=False):
        for (c0, nh, dst, dk) in vcopies:
            self.op("act", lambda e, c0=c0, nh=nh, dst=dst: e.activation(
                out=dst, in_=bank[:, c0:c0 + nh * 64].rearrange("p (h d) -> p h d", h=nh), func=AF.Copy),
                [bkey], [dk])
        for (c0, nh, blocks) in norms:
            r = self.rpu.next()
            w = nh * 64
            praw = self.praw[:, r, 0:w]
            sq = self.sqb[:, r, 0:w]
            qkn = self.qkn[:, r, 0:w]
            ssq = self.ssq[:, r, 0:nh]
            rsq = self.rsq[:, r, 0:nh]
            self.op("act", lambda e, c0=c0, w=w, praw=praw: e.activation(
                out=praw, in_=bank[:, c0:c0 + w], func=AF.Copy), [bkey], [("praw", r)])
            self.op("dve", lambda e, praw=praw, sq=sq: e.tensor_tensor(out=sq, in0=praw, in1=praw, op=ALU.mult),
                    [("praw", r)], [("sqb", r)])
            self.op("dve", lambda e, sq=sq, ssq=ssq, nh=nh: e.tensor_reduce(
                out=ssq, in_=sq.rearrange("p (h d) -> p h d", h=nh), axis=AX.X, op=ALU.add),
                [("sqb", r)], [("ssq", r)])
            self.rstd_chain(ssq, rsq, 64, [("ssq", r)], [("rsq", r)])
            if pair:
                self.op("dve", lambda e, praw=praw, qkn=qkn, rsq=rsq: e.tensor_tensor(
                    out=qkn.rearrange("p (b g d) -> p g b d", b=4, g=2),
                    in0=praw.rearrange("p (g b d) -> p g b d", g=2, b=4),
                    in1=rsq.rearrange("p (g b) -> p g b", g=2).unsqueeze(3).to_broadcast([128, 2, 4, 64]),
                    op=ALU.mult), [("praw", r), ("rsq", r)], [("qkn", r)])
            else:
                self.op("dve", lambda e, praw=praw, qkn=qkn, rsq=rsq, nh=nh: e.tensor_tensor(
                    out=qkn.rearrange("p (h d) -> p h d", h=nh), in0=praw.rearrange("p (h d) -> p h d", h=nh),
                    in1=rsq.unsqueeze(2).to_broadcast([128, nh, 64]), op=ALU.mult),
                    [("praw", r), ("rsq", r)], [("qkn", r)])
            bT, bk = self.bankT[r], ("bkT", r)
            q3 = qkn.rearrange("p (h d) -> p h d", h=nh)

            def finish(blocks=blocks, q3=q3, bT=bT, bk=bk, r=r):
                def tr(e):
                    for bi, (hs, dst, dk, sc) in enumerate(blocks):
                        ins = e.transpose(out=bT[:, bi * 128:(bi + 1) * 128],
                                          in_=q3[:, hs, :].rearrange("p h d -> p (h d)"), identity=self.ident[:])
                    return ins
                self.op("pe", tr, [("qkn", r), "ident"], [bk], lag=2)
                for bi, (hs, dst, dk, sc) in enumerate(blocks):
                    src = bT[:, bi * 128:(bi + 1) * 128]
                    if sc is None:
                        self.op("dve", lambda e, dst=dst, src=src: e.tensor_copy(out=dst, in_=src), [bk], [dk])
                    else:
                        self.op("act", lambda e, dst=dst, src=src, sc=sc: e.activation(
                            out=dst, in_=src, func=AF.Copy, scale=sc), [bk, "gs"], [dk])
            self.pu_flush()
            self.pu_pending = finish

    def pu_flush(self):
        if self.pu_pending is not None:
            f = self.pu_pending
            self.pu_pending = None
            f()

    def attn_unit(self, smm, skeys, negB_ap, emask, ekey, pvs, first, last):
        r = self.rpe4.next()
        pe_ = self.pexp[:, r, :]
        if pvs == "split":
            bSa, bSak = self.bank_S()
            bSb, bSbk = self.bank_S()
            self.op("pe", lambda e: smm(e, (bSa, bSb)), skeys, [bSak, bSbk])
            self.op("act", lambda e: e.activation(out=pe_[:, 0:256], in_=bSa[:, 0:256], func=AF.Exp,
                                                  bias=negB_ap, scale=1.0), [bSak, "negB"], [("pexp", r)])
            self.op("act", lambda e: e.activation(out=pe_[:, 256:512], in_=bSb[:, 0:256], func=AF.Exp,
                                                  bias=negB_ap, scale=1.0), [bSbk, "negB", ("pexp", r)],
                    [("pexp", r)])
        else:
            bS, bSk = self.bank_S()
            self.op("pe", lambda e: smm(e, bS), skeys, [bSk])
            self.op("act", lambda e: e.activation(out=pe_, in_=bS[:], func=AF.Exp, bias=negB_ap, scale=1.0),
                    [bSk, "negB"], [("pexp", r)])
        if emask is not None:
            self.op("dve", lambda e: e.tensor_tensor(out=pe_, in0=pe_, in1=emask, op=ALU.mult),
                    [("pexp", r), ekey], [("pexp", r)])
        return pe_, ("pexp", r)

    def build_enb(self, l):
        for h in range(4):
            ek = [("of32", 0, q) for q in range(4)]
            self.dma("pool", self.of32[:, 0, 0:896],
                     self.biasT_d[l, :, h * 896:(h + 1) * 896], [], ek, "est")
            self.op("act", lambda e: e.activation(out=self.of32[:, 0, 0:896], in_=self.of32[:, 0, 0:896],
                                                  func=AF.Exp), ek, ek)
            for v, (ji, mi) in enumerate(NB_VSPEC):
                self.op("dve", lambda e, v=v, ji=ji, mi=mi, h=h: e.tensor_tensor(
                    out=self.enb[:, v, (h % 2) * 2 + h // 2, :], in0=self.estage[:, ji, :],
                    in1=self.nmask[:, mi, :],
                    op=ALU.mult), ek + ["nmask"], ["enb"])

    def mem_kv(self, si, l):
        memd = self.mem_in[si]

        def nt_(mt):
            xk = [("xst", mt)]
            self.dma("pool", self.xst[:, mt, :], memd[mt * 128:(mt + 1) * 128, :], [], xk, ("xst", mt))
            self.norm_transpose(self.xst[:, mt, :], xk, self.hT[:, :, mt * 128:(mt + 1) * 128], [("hT", mt)])
        self.streams([lambda: nt_(0), lambda: nt_(1)])
        s = self.load_slab(self.wm_s[l].rearrange("p k n -> p (k n)"), 4096)
        wsl = self.wslab[:, s, :].rearrange("p (k n) -> p k n", k=8)
        gsm = self.gs[:, l * 3 + 2:l * 3 + 3]

        def pj(mt):
            bM, bMk = self.bank_M()

            def mm(e, mt=mt, bM=bM):
                for k in range(8):
                    ins = e.matmul(bM[:], lhsT=self.hT[:, k, mt * 128:(mt + 1) * 128], rhs=wsl[:, k, :],
                                   start=(k == 0), stop=(k == 7))
                return ins
            self.op("pe", mm, [("hT", mt), ("ws", s)], [bMk])
            blocks = [(slice(2 * b, 2 * b + 2), self.KmT[:, b, mt * 128:(mt + 1) * 128], "KmT", gsm)
                      for b in range(2)]
            self.proj_unit(bM, bMk, [(0, 4, blocks)], [(256, 4, self.Vm[:, mt, :, 0:64], "Vm")])
        pj(0)
        pj(1)
        self.pu_flush()

    def _layer_pass(self, si, S, l, src, dst, srck, dstk):
        nch = S // 512
        self.build_enb(l)
        self.mem_kv(si, l)
        for c in range(min(3, nch)):
            if c == 2:
                self.A_att(si, S, l, 0, src, srck)
                self.A_xnorm(0)
            self.P_stage(si, S, l, c, src, srck)
        if nch <= 2:
            self.A_att(si, S, l, 0, src, srck)
            self.A_xnorm(0)
        for i in range(nch):
            if i + 1 < nch:
                self.merge_prop(lambda: self.A_att(si, S, l, i + 1, src, srck), lambda: self.A_ff1(l, i))
            else:
                self.A_ff1(l, i)

            def y2(i=i):
                pre = None
                if i + 3 < nch:
                    pre = {sl: self.P_slab(l, sl) for sl in range(2)}
                if i + 1 < nch:
                    self.A_xnorm(i + 1)
                if i + 3 < nch:
                    self.P_stage(si, S, l, i + 3, src, srck, pre)
            self.merge_prop(y2, lambda: self.A_ff2(l, i, dst, dstk))

    def P_slab(self, l, sl):
        if sl < 3:
            s = self.load_slab(self.win_s[l, sl].rearrange("p k n -> p (k n)"), 4096)
            wsl = self.wslab[:, s, :].rearrange("p (k n) -> p k n", k=8)
        else:
            s = self.slab_slot("Y")
            wsl = self.wslab[:, s, 0:2048].rearrange("p (k n) -> p k n", k=8)
            self.dma("sp", wsl, self.win_s[l, 3, :, :, 0:256], self.wscr_keys, [("ws", s)], ("ws", s))
        return s, wsl

    def P_stage(self, si, S, l, i, src, srck, pre=None):
        qs, ks = i % 2, i % 3
        slabs = dict(pre) if pre else {}
        for sl in range(2):
            if sl not in slabs:
                slabs[sl] = self.P_slab(l, sl)

        items = []
        for t in range(4):
            ln = t % 2

            def pre(t=t, ln=ln):
                self.dma("pool", self.xst[:, ln, :], src[i * 512 + t * 128:i * 512 + (t + 1) * 128, :],
                         [srck + (i,)], [("xst", ln)], ("xst", ln))
            items.append((self.xst[:, ln, :], [("xst", ln)], self.hT[:, :, t * 128:(t + 1) * 128],
                          [("hT", t)], pre))
        self.norm_pipe(items)
        gsa = self.gs[:, l * 3 + 0:l * 3 + 1]
        gsb = self.gs[:, l * 3 + 1:l * 3 + 2]
        for sl in range(4):
            ncol = 512 if sl < 3 else 256
            if sl not in slabs:
                slabs[sl] = self.P_slab(l, sl)
            s, wsl = slabs[sl]

            def pj(t, sl=sl, s=s, wsl=wsl, ncol=ncol):
                bM, bMk = self.bank_M()
                tc = slice(t * 128, (t + 1) * 128)

                def mm(e):
                    for k in range(8):
                        ins = e.matmul(bM[:, 0:ncol], lhsT=self.hT[:, k, tc], rhs=wsl[:, k, :],
                                       start=(k == 0), stop=(k == 7))
                    return ins
                self.op("pe", mm, [("hT", t), ("ws", s)], [bMk])
                QTk, KTk, Vk = ("QT", qs, t), ("KT", ks, t), ("V", ks, t)
                if sl == 0:
                    blocks = [(slice(2 * b, 2 * b + 2), self.QT[:, qs, t, b, :], QTk, None) for b in range(4)]
                    self.proj_unit(bM, bMk, [(0, 8, blocks)], [], pair=True)
                elif sl == 1:
                    kb = [(slice(0, 2), self.KT[:, ks, 0, tc], KTk, gsa)]
                    qb = [(slice(2 * b, 2 * b + 2), self.QT[:, qs, t, 4 + b, :], QTk, None) for b in range(2)]
                    self.proj_unit(bM, bMk, [(0, 2, kb), (256, 4, qb)],
                                   [(128, 2, self.Vb[:, ks, t, 0:2, 0:64], Vk)])
                elif sl == 2:
                    kb = [(slice(2 * b, 2 * b + 2), self.KT[:, ks, 1 + b, tc], KTk, gsb) for b in range(2)]
                    self.proj_unit(bM, bMk, [(0, 4, kb)], [(256, 4, self.Vb[:, ks, t, 2:6, 0:64], Vk)])
                else:
                    qm = [(slice(2 * b, 2 * b + 2), self.QT[:, qs, t, 6 + b, :], QTk, None) for b in range(2)]
                    self.proj_unit(bM, bMk, [(0, 4, qm)], [])
            for t in range(4):
                pj(t)
        self.pu_flush()

    def A_att(self, si, S, l, i, src, srck):
        xs_ = i % 2
        self.dma("pool", self.xbuf[:, xs_], src[i * 512:(i + 1) * 512, :].rearrange("(t p) d -> p t d", p=128),
                 [srck + (i,)], [("xb", xs_, t) for t in range(4)], ("xb", xs_))
        nt = S // 128
        qs = i % 2
        negBa = self.negB[:, l * 3 + 0:l * 3 + 1]
        negBb = self.negB[:, l * 3 + 1:l * 3 + 2]
        negBm = self.negB[:, l * 3 + 2:l * 3 + 3]
        units = []
        ngrp = 0
        for t in range(4):
            T = 4 * i + t
            QTk = ("QT", qs, t)
            ln = t % 2
            for g in range(2):
                ob = ngrp % 2
                ngrp += 1
                bO, bOk = self.bankT[ob][:].bitcast(F32), ("bkT", ob)
                o3 = bO[:, 0:260].rearrange("p (h c) -> p h c", h=4)
                rels = [r for r in (-1, 0, 1) if 0 <= T + r < nt]
                ps = slice(64 * g, 64 * g + 64)
                for r_ in rels:
                    U = T + r_
                    cs, tu = (U // 4) % 3, U % 4
                    uc = slice(tu * 128, (tu + 1) * 128)

                    def smm(e, bS, cs=cs, uc=uc, ps=ps, t=t):
                        return e.matmul(bS[:], lhsT=self.KT[ps, cs, 0, uc],
                                        rhs=self.QT[ps, qs, t, 0:4, :].rearrange("p b n -> p (b n)"),
                                        start=True, stop=True)
                    first, last = (r_ == rels[0]), (r_ == rels[-1])

                    def pv(e, pt, cs=cs, tu=tu, first=first, last=last, o3=o3, g=g):
                        for hh in range(4):
                            ins = e.matmul(o3[:, hh, :], lhsT=pt[:, hh * 128:(hh + 1) * 128],
                                           rhs=self.Vb[:, cs, tu, g, :], start=(first and hh == 0), stop=last,
                                           skip_group_check=True)
                        return ins
                    fin = None
                    if last:
                        fin = (bO, bOk, o3, g * 256, self.sinkexp[:, l * 8 + 4 * g:l * 8 + 4 * g + 4],
                               "sinkexp", g, ln)
                    units.append(dict(smm=smm, skeys=[("KT", cs, tu), QTk], negB=negBa,
                                      emask=self.ea[:, g * 3 + r_ + 1, :], ekey="ea", split=False,
                                      pv=pv, pvkeys=[("V", cs, tu)], bOk=bOk, fin=fin, epi=None))
            if T == 0:
                case, js = "top0", [0, 1, 2, 3]
            elif T == 1:
                case, js = "top1", [-1, 0, 1, 2]
            elif T == nt - 2:
                case, js = "bot1", [-2, -1, 0, 1]
            elif T == nt - 1:
                case, js = "bot0", [-3, -2, -1, 0]
            else:
                case, js = "int", [-2, -1, 0, 1, 2]
            ob = ngrp % 2
            ngrp += 1
            bO, bOk = self.bankT[ob][:].bitcast(F32), ("bkT", ob)
            o3 = bO[:, 0:260].rearrange("p (h c) -> p h c", h=4)
            for j in js:
                U = T + j
                cs, tu = (U // 4) % 3, U % 4
                uc = slice(tu * 128, (tu + 1) * 128)
                v = nb_variant(case, j)

                def smm(e, bS, cs=cs, uc=uc, t=t):
                    for h in range(4):
                        ps = slice(64 * (h % 2), 64 * (h % 2) + 64)
                        ins = e.matmul(bS[h % 2][:, (h // 2) * 128:(h // 2 + 1) * 128],
                                       lhsT=self.KT[ps, cs, 1 + h // 2, uc],
                                       rhs=self.QT[ps, qs, t, 4 + h // 2, :], start=True, stop=True)
                    return ins
                first, last = (j == js[0]), (j == js[-1])

                def pv(e, pt, cs=cs, tu=tu, first=first, last=last, o3=o3):
                    for h in range(4):
                        hp = (h % 2) * 2 + h // 2
                        ins = e.matmul(o3[:, h, :], lhsT=pt[:, hp * 128:(hp + 1) * 128],
                                       rhs=self.Vb[:, cs, tu, 2 + h, :], start=(first and h == 0), stop=last,
                                       skip_group_check=True)
                    return ins
                fin = (bO, bOk, o3, 512, None, None, 2, ln) if last else None
                units.append(dict(smm=smm, skeys=[("KT", cs, tu), QTk], negB=negBb,
                                  emask=self.enb[:, v].rearrange("p h n -> p (h n)"), ekey="enb", split=True,
                                  pv=pv, pvkeys=[("V", cs, tu)], bOk=bOk, fin=fin, epi=None))
            ob = ngrp % 2
            ngrp += 1
            bO, bOk = self.bankT[ob][:].bitcast(F32), ("bkT", ob)
            o3 = bO[:, 0:260].rearrange("p (h c) -> p h c", h=4)
            for mt in range(2):
                mc = slice(mt * 128, (mt + 1) * 128)

                def smm(e, bS, mc=mc, t=t):
                    for h in range(4):
                        ps = slice(64 * (h % 2), 64 * (h % 2) + 64)
                        ins = e.matmul(bS[h % 2][:, (h // 2) * 128:(h // 2 + 1) * 128],
                                       lhsT=self.KmT[ps, h // 2, mc],
                                       rhs=self.QT[ps, qs, t, 6 + h // 2, :], start=True, stop=True)
                    return ins

                def pv(e, pt, mt=mt, o3=o3):
                    for h in range(4):
                        hp = (h % 2) * 2 + h // 2
                        ins = e.matmul(o3[:, h, :], lhsT=pt[:, hp * 128:(hp + 1) * 128],
                                       rhs=self.Vm[:, mt, h, :], start=(mt == 0 and h == 0), stop=(mt == 1),
                                       skip_group_check=True)
                    return ins
                fin = (bO, bOk, o3, 768, None, None, 3, ln) if mt == 1 else None
                units.append(dict(smm=smm, skeys=["KmT", QTk], negB=negBm, emask=None, ekey=None, split=True,
                                  pv=pv, pvkeys=["Vm"], bOk=bOk, fin=fin, epi=(t if mt == 1 else None)))

        def back(u):
            pt, ptk = u["pt"]
            self.op("pe", lambda e, u=u, pt=pt: u["pv"](e, pt), [ptk] + u["pvkeys"], [u["bOk"]], lag=1)
            if u["fin"] is not None:
                self.finish_heads(*u["fin"])
            for pe_ in pend_epi:
                pe_[0] -= 1
            while pend_epi and pend_epi[0][0] <= 0:
                pend_epi.pop(0)[1]()
            if u["epi"] is not None:
                pend_epi.append([3, self.attn_epilogue(u["epi"])])
        pend_epi = []
        prev = None
        for u in units:
            u["pt"] = self.attn_unit(u["smm"], u["skeys"], u["negB"], u["emask"], u["ekey"],
                                     "split" if u["split"] else None, None, None)
            if prev is not None:
                back(prev)
            prev = u
        back(prev)
        for pe_ in pend_epi:
            pe_[1]()
        for n in range(2):
            s = self.load_slab(self.wout_s[l, n].rearrange("p k n -> p (k n)"), 4096)
            wsl = self.wslab[:, s, :].rearrange("p (k n) -> p k n", k=8)
            for t in range(4):
                kb = self.rOP.next()
                bM, bMk = self.bankF[kb], ("bk", kb)
                tc = slice(t * 128, (t + 1) * 128)

                def mm(e, bM=bM, wsl=wsl, tc=tc):
                    for k in range(8):
                        ins = e.matmul(bM[:], lhsT=self.hT[:, k, tc], rhs=wsl[:, k, :], start=(k == 0),
                                       stop=(k == 7))
                    return ins
                self.op("pe", mm, [("hT", t), ("ws", s)], [bMk])
                xv = self.xbuf[:, xs_, t, n * 512:(n + 1) * 512]
                self.op("dve", lambda e, xv=xv, bM=bM: e.tensor_tensor(out=xv, in0=bM[:], in1=xv, op=ALU.add),
                        [bMk, ("xb", xs_, t)], [("xb", xs_, t)])

    def A_xnorm(self, i):
        xs_ = i % 2

        self.norm_pipe([(self.xbuf[:, xs_, t, :], [("xb", xs_, t)], self.oxT[:, :, t * 128:(t + 1) * 128],
                         [("oxT", t)], None) for t in range(4)])

    def attn_epilogue(self, t):
        ln = t % 2
        tc = slice(t * 128, (t + 1) * 128)
        of32, obf, sso = self.of32[:, ln, :], self.obf[:, ln, :], self.sso[:, ln, :]
        for gi, (c0, w) in enumerate(((0, 512), (512, 256), (768, 256))):
            rk = [("of32", ln, 0), ("of32", ln, 1)] if gi == 0 else [("of32", ln, gi + 1)]
            self.op("act", lambda e, c0=c0, w=w, gi=gi: e.activation(
                out=obf[:, c0:c0 + w], in_=of32[:, c0:c0 + w], func=AF.Square,
                accum_out=sso[:, gi:gi + 1]), rk, [("sso", ln, gi), ("obf", ln, gi)])
        for gi, (c0, w) in enumerate(((0, 512), (512, 256), (768, 256))):
            self.rstd_chain(sso[:, gi:gi + 1], sso[:, gi:gi + 1], w, [("sso", ln, gi)], [("sso", ln, gi)])
            self.op("dve", lambda e, c0=c0, w=w, gi=gi: e.tensor_scalar(
                out=obf[:, c0:c0 + w], in0=of32[:, c0:c0 + w], scalar1=sso[:, gi:gi + 1],
                scalar2=None, op0=ALU.mult), [("sso", ln, gi)] + [("of32", ln, q) for q in range(4)],
                [("obf", ln, gi)])
        def back():
            kb = self.rS4.next()
            bT, bk = self.bankF[kb][:].bitcast(BF16), ("bk", kb)

            def tr(e, bT=bT):
                for k in range(8):
                    ins = e.transpose(out=bT[:, k * 128:(k + 1) * 128], in_=obf[:, k * 128:(k + 1) * 128],
                                      identity=self.ident[:])
                return ins
            self.op("pe", tr, [("obf", ln, 0), ("obf", ln, 1), ("obf", ln, 2), "ident"], [bk], lag=2)
            self.op("act", lambda e, bT=bT, tc=tc: e.activation(
                out=self.hT[:, :, tc], in_=bT.rearrange("p (k n) -> p k n", k=8), func=AF.Copy),
                [bk], [("hT", t)])
        return back

    def A_ff1(self, l, i):
        oxk = [("oxT", t) for t in range(4)]
        for sl in range(8):
            s = self.load_slab(self.w1_s[l, :, sl].rearrange("p j k h -> p (j k h)"), 4096, "X")
            wsl = self.wslab[:, s, :].rearrange("p (j k h) -> p j k h", j=4, k=8)
            for jj in range(4):
                j = sl * 4 + jj
                kb = 4 + self.rF1.next()
                bM, bMk = self.bankF[kb], ("bk", kb)

                for half in range(2):
                    def mm(e, bM=bM, wsl=wsl, jj=jj, half=half):
                        for k in range(4 * half, 4 * half + 4):
                            ins = e.matmul(bM[:], lhsT=wsl[:, jj, k, :], rhs=self.oxT[:, k, :], start=(k == 0),
                                           stop=(k == 7))
                        return ins
                    self.op("pe", mm, oxk + [("ws", s)], [bMk])
                r = self.rrt.next()
                rt = self.rtmp[:, r, :]
                self.op("act", lambda e, rt=rt, bM=bM: e.activation(out=rt, in_=bM[:], func=AF.Relu),
                        [bMk], [("rtmp", r)])
                self.op("pool", lambda e, rt=rt, j=j: e.tensor_tensor(out=self.gT[:, j, :], in0=rt, in1=rt,
                                                                      op=ALU.mult),
                        [("rtmp", r)], [("gT", j)])

    def A_ff2(self, l, i, dst, dstk):
        xs_ = i % 2
        G = [(self.bankF[2 + t], ("bk", 2 + t)) for t in range(4)]
        for n in range(2):
            for ns in range(4):
                s = self.load_slab(self.w2_s[l, n, ns].rearrange("p j c -> p (j c)"), 4096, "X")
                wsl = self.wslab[:, s, :].rearrange("p (j c) -> p j c", j=8)

                for jj in range(8):
                    j = ns * 8 + jj

                    def mm(e, wsl=wsl, j=j, jj=jj):
                        for t in range(4):
                            ins = e.matmul(G[t][0][:], lhsT=self.gT[:, j, t * 128:(t + 1) * 128],
                                           rhs=wsl[:, jj, :], start=(j == 0), stop=(j == 31))
                        return ins
                    self.op("pe", mm, [("gT", j), ("ws", s)], [G[t][1] for t in range(4)])
            for t in range(4):
                xv = self.xbuf[:, xs_, t, n * 512:(n + 1) * 512]
                self.op("dve", lambda e, xv=xv, b=G[t][0]: e.tensor_tensor(out=xv, in0=b[:], in1=xv, op=ALU.add),
                        [G[t][1], ("xb", xs_, t)], [("xb", xs_, t)])
        self.dma("pool", dst[i * 512:(i + 1) * 512, :].rearrange("(t p) d -> p t d", p=128), self.xbuf[:, xs_],
                 [("xb", xs_, t) for t in range(4)], [dstk + (i,)], ("st", xs_))

    def finish_heads(self, bO, bOk, o3, col0, sink_ap, sink_key, gi, ln):
        den = self.dtmp[:, ln, gi, :]
        dk = ("den", ln, gi)
        if sink_ap is not None:
            self.op("dve", lambda e: e.tensor_tensor(out=den, in0=o3[:, :, 64], in1=sink_ap, op=ALU.add),
                    [bOk, sink_key], [dk])
        else:
            self.op("dve", lambda e: e.tensor_scalar(out=den, in0=o3[:, :, 64], scalar1=0.0, scalar2=None,
                                                     op0=ALU.add), [bOk], [dk])
        self.op("dve", lambda e: e.reciprocal(out=den, in_=den), [dk], [dk])
        self.op("dve", lambda e: e.tensor_tensor(
            out=self.of32[:, ln, col0:col0 + 256].rearrange("p (h d) -> p h d", h=4), in0=o3[:, :, 0:64],
            in1=den.unsqueeze(2).to_broadcast([128, 4, 64]), op=ALU.mult),
            [bOk, dk], [("of32", ln, gi)])


def _const_ea():
    kj = np.arange(128)[:, None]
    qi = np.arange(128)[None, :]
    out = np.zeros((128, 6, 4, 128), np.float32)
    for g in range(2):
        for hh in range(4):
            slope = 2.0 ** (-(4 * g + hh + 1))
            for ri, rel in enumerate((-1, 0, 1)):
                if rel == -1:
                    dist = 128 + qi - kj
                elif rel == 0:
                    dist = np.abs(qi - kj)
                else:
                    dist = 128 + kj - qi
                valid = dist <= 128
                out[:, g * 3 + ri, hh, :] = np.where(valid, np.exp(-slope * dist), 0.0)
    return out.reshape(128, 6 * 512).astype(ml_dtypes.bfloat16)


def _const_nmask():
    out = np.zeros((128, NMK, 128), np.float32)
    c = np.arange(64)
    cstart = np.clip(c - 8, 0, 48)
    colvalid = (c[None, :] >= cstart[:, None]) & (c[None, :] < cstart[:, None] + 16)
    for mi, j in enumerate((None, -2, 2)):
        for a in range(2):
            for b in range(2):
                ok = True if j is None else (-4 <= 2 * j + b - a <= 3)
                if ok:
                    out[b * 64:(b + 1) * 64, mi, a * 64:(a + 1) * 64] = colvalid.T.astype(np.float32)
    return out.reshape(128, NMK * 128).astype(ml_dtypes.bfloat16)


def _layout_biasT(rpb):
    L = rpb.shape[0]
    b = np.arange(2)[:, None, None, None, None]
    kc = np.arange(64)[None, :, None, None, None]
    j = np.arange(-3, 4)[None, None, :, None, None]
    a = np.arange(2)[None, None, None, :, None]
    qc = np.arange(64)[None, None, None, None, :]
    dr = np.clip(2 * j + b - a + 7, 0, 14) + 0 * kc + 0 * qc
    dc = np.clip(kc - qc + 15, 0, 30) + 0 * b + 0 * j + 0 * a
    g = rpb[:, :, dr, dc]
    g = np.transpose(g, (0, 2, 3, 1, 4, 5, 6))
    return np.ascontiguousarray(g.reshape(L, 128, 4 * 7 * 128)).astype(np.float32)


def _shared_inputs(g_mix, w_in, qk_gain, sink, rpb, o_gain, w_out, g_mem, w_mem_kv, g_ff, w_ff1, w_ff2):
    L = w_in.shape[0]
    gvs = np.stack([g_mix, o_gain, g_mem, g_ff], axis=1)
    gcols = gvs.reshape(L, 4, 8, 128).transpose(3, 0, 1, 2).reshape(128, L * 4 * 8)
    gaincols = np.tile(qk_gain.transpose(2, 0, 1).reshape(64, L * 6), (2, 1))
    return {
        "w_in": np.ascontiguousarray(w_in, np.float32), "w_out": np.ascontiguousarray(w_out, np.float32),
        "w_mem_kv": np.ascontiguousarray(w_mem_kv, np.float32),
        "w_ff1": np.ascontiguousarray(w_ff1, np.float32), "w_ff2": np.ascontiguousarray(w_ff2, np.float32),
        "gcols": np.ascontiguousarray(gcols, np.float32),
        "gaincols": np.ascontiguousarray(gaincols, np.float32),
        "gainrows": np.ascontiguousarray(qk_gain.reshape(1, L * 6 * 64), np.float32),
        "sinkrow": np.ascontiguousarray(sink.reshape(1, L * 8), np.float32),
        "biasT": _layout_biasT(np.asarray(rpb, np.float32)),
        "ident": np.eye(128, dtype=np.float32).astype(ml_dtypes.bfloat16),
        "ea": _const_ea(), "nmask": _const_nmask(),
    }


_NC_CACHE = {}


def _get_nc(segs, depth):
    key = (tuple(segs), depth)
    if key not in _NC_CACHE:
        _NC_CACHE[key] = Builder(list(segs), depth).build()
    return _NC_CACHE[key]


def kernel(x_prompt, x_sample, mem_prompt, mem_sample, g_mix, w_in, qk_gain, sink, rpb,
           o_gain, w_out, g_mem, w_mem_kv, g_ff, w_ff1, w_ff2):
    x_prompt = np.asarray(x_prompt, np.float32)
    x_sample = np.asarray(x_sample, np.float32)
    mem_prompt = np.asarray(mem_prompt, np.float32)
    mem_sample = np.asarray(mem_sample, np.float32)
    BP, SP = x_prompt.shape[0], x_prompt.shape[1]
    BS, SS = x_sample.shape[0], x_sample.shape[1]
    depth = w_in.shape[0]
    assert BP == NCORES and BS * 2 == NCORES and depth == 2
    HALF = SS // 2
    HALO = 512
    SEG = HALF + HALO
    segs = (("p", SP), ("s", SEG))
    nc = _get_nc(segs, depth)
    shared = _shared_inputs(*[np.asarray(a, np.float32) for a in
                              (g_mix, w_in, qk_gain, sink, rpb, o_gain, w_out, g_mem, w_mem_kv, g_ff,
                               w_ff1, w_ff2)])
    in_maps = []
    for c in range(NCORES):
        b, h = c % BS, c // BS
        lo = 0 if h == 0 else SS - SEG
        m = dict(shared)
        m["x_p"] = np.ascontiguousarray(x_prompt[c])
        m["mem_p"] = np.ascontiguousarray(mem_prompt[c])
        m["x_s"] = np.ascontiguousarray(x_sample[b, lo:lo + SEG])
        m["mem_s"] = np.ascontiguousarray(mem_sample[b])
        in_maps.append(m)
    res = run_bass_kernel_spmd(nc, in_maps, core_ids=list(range(NCORES)))
    y_p = np.stack([np.asarray(res.results[c]["y_p"], np.float32) for c in range(BP)], axis=0)
    y_s = np.empty((BS, SS, D), np.float32)
    for c in range(NCORES):
        b, h = c % BS, c // BS
        ys = np.asarray(res.results[c]["y_s"], np.float32)
        if h == 0:
            y_s[b, 0:HALF] = ys[0:HALF]
        else:
            y_s[b, HALF:SS] = ys[SEG - HALF:SEG]
    return (y_p, y_s)
```
